# Optimizing a Trainium2 kernel written in Bass

```python
import math
import jax, jax.numpy as jnp
from jax import lax
import numpy as np


D_MODEL = 1024
BATCH = 2
SEQ = 8192
DEPTH = 2
DEC_BATCH = 128
DEC_SEQ = 8
PAST_LEN = 2048
PAGE_SIZE = 128

N_A_LAYERS = DEPTH // 2
N_B_LAYERS = DEPTH - N_A_LAYERS
SSM_GROUP = 16
N_GROUPS = D_MODEL // SSM_GROUP
SSM_STATE = 64
SSM_CHUNK = 128
DT_MIN = 1e-3
DT_MAX = 1e-1
N_HEADS = 8
HEAD_DIM = 64
V_DIM = 2 * HEAD_DIM
QK_WIDTH = N_HEADS * 2 * HEAD_DIM
V_WIDTH = N_HEADS * V_DIM
D_FF = 2816
CONV_W = 3
Q_BLOCK = 128
NORM_EPS = 1e-6

kernel_name = 'yoco_s5_diffattn_convffn_step'


def rmsnorm(x, g):
    xf = x.astype(jnp.float32)
    y = xf * lax.rsqrt(jnp.mean(xf * xf, axis=-1, keepdims=True) + NORM_EPS)
    return (y * g.astype(jnp.float32)).astype(x.dtype)


def _ssm_combine(e1, e2):
    a1, b1 = e1
    a2, b2 = e2
    return a1 * a2, a2 * b1 + b2


def s5_mixer(u, h0, lam_re, lam_im, log_dt, b_re, b_im, c_re, c_im, d_skip, w_glu):
    bsz, seq_len, _ = u.shape
    lam = lax.complex(lam_re.astype(jnp.float32), lam_im.astype(jnp.float32))
    dt = jnp.exp(log_dt.astype(jnp.float32))[:, None]
    lam_dt = lam * dt
    lam_bar = jnp.exp(lam_dt)
    b = lax.complex(b_re.astype(jnp.float32), b_im.astype(jnp.float32))
    b_bar = ((lam_bar - 1.0) / lam)[..., None] * b
    c = lax.complex(c_re.astype(jnp.float32), c_im.astype(jnp.float32))
    chunk = SSM_CHUNK if seq_len % SSM_CHUNK == 0 else seq_len
    n_chunks = seq_len // chunk
    steps = jnp.arange(1, chunk + 1, dtype=jnp.float32)[:, None, None]
    pows = jnp.exp(lam_dt[None] * steps)
    uf = u.astype(jnp.float32).reshape(bsz, n_chunks, chunk, N_GROUPS, SSM_GROUP)
    u_chunks = jnp.moveaxis(uf, 1, 0)

    def step(h_prev, u_c):
        bu = jnp.einsum('blgc,gpc->blgp', u_c.astype(jnp.complex64), b_bar)
        a = jnp.broadcast_to(lam_bar, bu.shape)
        _, h_zero = lax.associative_scan(_ssm_combine, (a, bu), axis=1)
        h = pows[None] * h_prev[:, None] + h_zero
        y_c = jnp.real(jnp.einsum('blgp,gcp->blgc', h, c))
        return h[:, -1], y_c

    h_last, ys = lax.scan(step, h0, u_chunks)
    y = jnp.moveaxis(ys, 0, 1).reshape(bsz, seq_len, D_MODEL)
    y = y + d_skip.astype(jnp.float32) * u.astype(jnp.float32)
    z = jax.nn.gelu(y).astype(u.dtype)
    gl = z @ w_glu
    out = gl[..., :D_MODEL] * jax.nn.sigmoid(gl[..., D_MODEL:])
    return out, h_last


def conv_ffn(x, buf, w_up, conv_w, conv_b, w_down):
    seq_len = x.shape[1]
    h = x @ w_up
    hp = jnp.concatenate([buf.astype(h.dtype), h], axis=1)
    hc = conv_b + conv_w[0] * hp[:, 0:seq_len]
    for tap in range(1, CONV_W):
        hc = hc + conv_w[tap] * hp[:, tap:tap + seq_len]
    gate, val = jnp.split(hc, 2, axis=-1)
    out = (jax.nn.gelu(gate) * val) @ w_down
    return out, hp[:, -(CONV_W - 1):]


def diff_attention(q, k, v, q_offset, lam, sub_g, lam_init):
    bsz, seq_len = q.shape[:2]
    n_keys = k.shape[1]
    qb = Q_BLOCK if seq_len % Q_BLOCK == 0 else seq_len
    nb = seq_len // qb
    slopes = 2.0 ** (-8.0 * jnp.arange(1, N_HEADS + 1, dtype=jnp.float32) / N_HEADS)
    k_pos = jnp.arange(n_keys, dtype=jnp.int32)
    scale = HEAD_DIM ** -0.5
    q_blocks = jnp.moveaxis(q.reshape(bsz, nb, qb, N_HEADS, 2, HEAD_DIM), 1, 0)

    def block(args):
        i, q_blk = args
        q_pos = q_offset + i * qb + jnp.arange(qb, dtype=jnp.int32)
        s = jnp.einsum('bqhjd,bkhjd->bhjqk', q_blk, k, preferred_element_type=jnp.float32) * scale
        dist = (q_pos[:, None] - k_pos[None, :]).astype(jnp.float32)
        s = s - slopes[:, None, None, None] * dist
        s = jnp.where(k_pos[None, :] <= q_pos[:, None], s, -jnp.inf)
        p = jax.nn.softmax(s, axis=-1)
        w = p[:, :, 0] - lam * p[:, :, 1]
        return jnp.einsum('bhqk,bkhe->bqhe', w.astype(v.dtype), v)

    o = lax.map(block, (jnp.arange(nb, dtype=jnp.int32), q_blocks))
    o = jnp.moveaxis(o, 0, 1).reshape(bsz, seq_len, N_HEADS, V_DIM)
    o = rmsnorm(o, sub_g) * (1.0 - lam_init)
    return o.reshape(bsz, seq_len, V_WIDTH)


def trunk(x, ssm_h0, conv_buf0, past_k, past_v, p):
    bsz, seq_len, _ = x.shape
    ssm_out = []
    conv_out = []
    k_all = v_all = k_new = v_new = None
    for i in range(DEPTH):
        if i < N_A_LAYERS:
            a_out, h_last = s5_mixer(rmsnorm(x, p['a_pre_g'][i]), ssm_h0[i], p['ssm_lam_re'][i], p['ssm_lam_im'][i],
                                     p['ssm_log_dt'][i], p['ssm_b_re'][i], p['ssm_b_im'][i], p['ssm_c_re'][i],
                                     p['ssm_c_im'][i], p['ssm_d'][i], p['glu_w'][i])
            x = x + rmsnorm(a_out, p['a_post_g'][i])
            ssm_out.append(h_last)
        else:
            j = i - N_A_LAYERS
            if j == 0:
                kv_in = rmsnorm(x, p['kv_norm_g'])
                k_new = (kv_in @ p['w_k']).reshape(bsz, seq_len, N_HEADS, 2 * HEAD_DIM)
                v_new = (kv_in @ p['w_v']).reshape(bsz, seq_len, N_HEADS, V_DIM)
                if past_k is None:
                    k_all, v_all = k_new, v_new
                else:
                    k_all = jnp.concatenate([past_k.astype(k_new.dtype), k_new], axis=1)
                    v_all = jnp.concatenate([past_v.astype(v_new.dtype), v_new], axis=1)
            n_keys = k_all.shape[1]
            lam_init = 0.8 - 0.6 * math.exp(-0.3 * i)
            lam = (jnp.exp(jnp.sum(p['lam_q1'][j].astype(jnp.float32) * p['lam_k1'][j].astype(jnp.float32)))
                   - jnp.exp(jnp.sum(p['lam_q2'][j].astype(jnp.float32) * p['lam_k2'][j].astype(jnp.float32)))
                   + lam_init)
            xn = rmsnorm(x, p['b_pre_g'][j])
            q = (xn @ p['w_q'][j]).reshape(bsz, seq_len, N_HEADS, 2, HEAD_DIM)
            o = diff_attention(q, k_all.reshape(bsz, n_keys, N_HEADS, 2, HEAD_DIM), v_all,
                               n_keys - seq_len, lam, p['sub_g'][j], lam_init)
            x = x + rmsnorm(o @ p['w_o'][j], p['b_post_g'][j])
        f_out, buf = conv_ffn(rmsnorm(x, p['f_pre_g'][i]), conv_buf0[i], p['w_up'][i], p['conv_w'][i],
                              p['conv_b'][i], p['w_down'][i])
        x = x + rmsnorm(f_out, p['f_post_g'][i])
        conv_out.append(buf)
    return x, jnp.stack(ssm_out), jnp.stack(conv_out), k_new, v_new


def setup_inputs(seed: int = 0) -> dict:
    key = jax.random.key(seed)
    ks = iter(jax.random.split(key, 48))

    def nrm(shape, scale):
        return jax.random.normal(next(ks), shape, jnp.float32) * scale

    def gain(shape):
        return 1.0 + 0.02 * jax.random.normal(next(ks), shape, jnp.float32)

    n_pages = PAST_LEN // PAGE_SIZE
    n_used = DEC_BATCH * n_pages
    n_pool = n_used + max(1, n_used // 4)
    F2 = 2 * D_FF
    return {
        'x_prompt': nrm((BATCH, SEQ, D_MODEL), 1.0),
        'x_sample': nrm((DEC_BATCH, DEC_SEQ, D_MODEL), 1.0),
        'state_ssm_re': nrm((N_A_LAYERS, DEC_BATCH, N_GROUPS, SSM_STATE), 0.1),
        'state_ssm_im': nrm((N_A_LAYERS, DEC_BATCH, N_GROUPS, SSM_STATE), 0.1),
        'state_conv': nrm((DEPTH, DEC_BATCH, CONV_W - 1, F2), 1.0),
        'cache_k': nrm((n_pool, PAGE_SIZE, N_HEADS, 2 * HEAD_DIM), 1.0),
        'cache_v': nrm((n_pool, PAGE_SIZE, N_HEADS, V_DIM), 1.0),
        'page_table': jax.random.permutation(next(ks), n_pool)[:n_used].reshape(DEC_BATCH, n_pages).astype(jnp.int32),
        'a_pre_g': gain((N_A_LAYERS, D_MODEL)),
        'a_post_g': gain((N_A_LAYERS, D_MODEL)),
        'ssm_lam_re': -0.5 + nrm((N_A_LAYERS, N_GROUPS, SSM_STATE), 0.01),
        'ssm_lam_im': jnp.pi * jnp.arange(SSM_STATE, dtype=jnp.float32) + nrm((N_A_LAYERS, N_GROUPS, SSM_STATE), 0.01),
        'ssm_log_dt': jax.random.uniform(next(ks), (N_A_LAYERS, N_GROUPS), jnp.float32,
                                         minval=math.log(DT_MIN), maxval=math.log(DT_MAX)),
        'ssm_b_re': nrm((N_A_LAYERS, N_GROUPS, SSM_STATE, SSM_GROUP), (2 * SSM_GROUP) ** -0.5),
        'ssm_b_im': nrm((N_A_LAYERS, N_GROUPS, SSM_STATE, SSM_GROUP), (2 * SSM_GROUP) ** -0.5),
        'ssm_c_re': nrm((N_A_LAYERS, N_GROUPS, SSM_GROUP, SSM_STATE), (2 * SSM_STATE) ** -0.5),
        'ssm_c_im': nrm((N_A_LAYERS, N_GROUPS, SSM_GROUP, SSM_STATE), (2 * SSM_STATE) ** -0.5),
        'ssm_d': nrm((N_A_LAYERS, D_MODEL), 1.0),
        'glu_w': nrm((N_A_LAYERS, D_MODEL, 2 * D_MODEL), D_MODEL ** -0.5),
        'kv_norm_g': gain((D_MODEL,)),
        'w_k': nrm((D_MODEL, QK_WIDTH), D_MODEL ** -0.5),
        'w_v': nrm((D_MODEL, V_WIDTH), D_MODEL ** -0.5),
        'b_pre_g': gain((N_B_LAYERS, D_MODEL)),
        'b_post_g': gain((N_B_LAYERS, D_MODEL)),
        'w_q': nrm((N_B_LAYERS, D_MODEL, QK_WIDTH), D_MODEL ** -0.5),
        'lam_q1': nrm((N_B_LAYERS, HEAD_DIM), 0.1),
        'lam_k1': nrm((N_B_LAYERS, HEAD_DIM), 0.1),
        'lam_q2': nrm((N_B_LAYERS, HEAD_DIM), 0.1),
        'lam_k2': nrm((N_B_LAYERS, HEAD_DIM), 0.1),
        'sub_g': gain((N_B_LAYERS, V_DIM)),
        'w_o': nrm((N_B_LAYERS, V_WIDTH, D_MODEL), V_WIDTH ** -0.5),
        'f_pre_g': gain((DEPTH, D_MODEL)),
        'f_post_g': gain((DEPTH, D_MODEL)),
        'w_up': nrm((DEPTH, D_MODEL, F2), D_MODEL ** -0.5),
        'conv_w': nrm((DEPTH, CONV_W, F2), CONV_W ** -0.5),
        'conv_b': nrm((DEPTH, F2), 0.01),
        'w_down': nrm((DEPTH, D_FF, D_MODEL), D_FF ** -0.5),
    }


def reference(x_prompt, x_sample, state_ssm_re, state_ssm_im, state_conv, cache_k, cache_v, page_table,
              a_pre_g, a_post_g, ssm_lam_re, ssm_lam_im, ssm_log_dt, ssm_b_re, ssm_b_im, ssm_c_re, ssm_c_im,
              ssm_d, glu_w, kv_norm_g, w_k, w_v, b_pre_g, b_post_g, w_q, lam_q1, lam_k1, lam_q2, lam_k2,
              sub_g, w_o, f_pre_g, f_post_g, w_up, conv_w, conv_b, w_down):
    p = {'a_pre_g': a_pre_g, 'a_post_g': a_post_g, 'ssm_lam_re': ssm_lam_re, 'ssm_lam_im': ssm_lam_im,
         'ssm_log_dt': ssm_log_dt, 'ssm_b_re': ssm_b_re, 'ssm_b_im': ssm_b_im, 'ssm_c_re': ssm_c_re,
         'ssm_c_im': ssm_c_im, 'ssm_d': ssm_d, 'glu_w': glu_w, 'kv_norm_g': kv_norm_g, 'w_k': w_k, 'w_v': w_v,
         'b_pre_g': b_pre_g, 'b_post_g': b_post_g, 'w_q': w_q, 'lam_q1': lam_q1, 'lam_k1': lam_k1,
         'lam_q2': lam_q2, 'lam_k2': lam_k2, 'sub_g': sub_g, 'w_o': w_o, 'f_pre_g': f_pre_g,
         'f_post_g': f_post_g, 'w_up': w_up, 'conv_w': conv_w, 'conv_b': conv_b, 'w_down': w_down}
    st_dtype = state_ssm_re.dtype

    bsz = x_prompt.shape[0]
    h0_p = jnp.zeros((N_A_LAYERS, bsz, N_GROUPS, SSM_STATE), jnp.complex64)
    conv0_p = jnp.zeros((DEPTH, bsz, CONV_W - 1, 2 * D_FF), x_prompt.dtype)
    y_prompt, ssm_p, conv_prompt, k_prompt, v_prompt = trunk(x_prompt, h0_p, conv0_p, None, None, p)

    dec_b, n_pages = page_table.shape
    past_len = n_pages * cache_k.shape[1]
    past_k = cache_k[page_table].reshape(dec_b, past_len, N_HEADS, 2 * HEAD_DIM)
    past_v = cache_v[page_table].reshape(dec_b, past_len, N_HEADS, V_DIM)
    h0_s = lax.complex(state_ssm_re.astype(jnp.float32), state_ssm_im.astype(jnp.float32))
    y_sample, ssm_s, conv_sample, k_sample, v_sample = trunk(x_sample, h0_s, state_conv, past_k, past_v, p)

    ssm_re_prompt = jnp.real(ssm_p).astype(st_dtype)
    ssm_im_prompt = jnp.imag(ssm_p).astype(st_dtype)
    ssm_re_sample = jnp.real(ssm_s).astype(st_dtype)
    ssm_im_sample = jnp.imag(ssm_s).astype(st_dtype)
    return (y_prompt, y_sample, ssm_re_prompt, ssm_im_prompt, conv_prompt, k_prompt, v_prompt,
            ssm_re_sample, ssm_im_sample, conv_sample, k_sample, v_sample)
```

```python
import contextlib
import numpy as np
import concourse.bass as bass
import concourse.mybir as mybir
from concourse.bass_utils import run_bass_kernel_spmd

F32 = mybir.dt.float32
BF16 = mybir.dt.bfloat16
I32 = mybir.dt.int32
U32 = mybir.dt.uint32
AF = mybir.ActivationFunctionType
ALU = mybir.AluOpType
AX = mybir.AxisListType

NDMA_SEM = 16


class Buf:
    __slots__ = ("name", "writer", "readers")

    def __init__(self, name=""):
        self.name = name
        self.writer = None
        self.readers = {}


class Instr:
    __slots__ = ("eng", "fn", "waits", "idx", "is_dma", "dma_slot", "signal", "semval")

    def __init__(self, eng, fn, is_dma):
        self.eng = eng
        self.fn = fn
        self.waits = []
        self.is_dma = is_dma
        self.dma_slot = None
        self.signal = False
        self.semval = None


class Prog:
    COMPUTE = ("pe", "act", "dve", "pool")

    def __init__(self, nc):
        self.nc = nc
        self.streams = {k: [] for k in ("pe", "act", "dve", "pool", "sp")}
        self.dma_count = {"sp": 0, "pool": 0, "act": 0}
        self.all = []

    def _add(self, eng, fn, reads, writes, is_dma=False):
        ins = Instr(eng, fn, is_dma)
        deps = []
        for b in reads:
            if b.writer is not None:
                deps.append(b.writer)
        for b in writes:
            if b.writer is not None:
                deps.append(b.writer)
            for e, ev in b.readers.items():
                deps.append(ev)
        seen = set()
        for d in deps:
            if id(d) in seen or d is ins:
                continue
            seen.add(id(d))
            if (not d.is_dma) and (not is_dma) and d.eng == eng and eng == "pe":
                continue
            ins.waits.append(d)
        if is_dma:
            n = self.dma_count[eng]
            self.dma_count[eng] = n + 1
            ins.dma_slot = n
        ins.idx = len(self.streams[eng])
        self.streams[eng].append(ins)
        for b in reads:
            b.readers[eng if not is_dma else ("dma", id(ins))] = ins
        for b in writes:
            b.writer = ins
            b.readers = {}
        return ins

    def op(self, eng, fn, reads=(), writes=()):
        return self._add(eng, fn, list(reads), list(writes), False)

    def dma(self, eng, out, in_, reads=(), writes=(), **kw):
        return self._add(eng, lambda e: e.dma_start(out=out, in_=in_, **kw), list(reads), list(writes), True)

    def dma_fn(self, eng, fn, reads=(), writes=()):
        return self._add(eng, fn, list(reads), list(writes), True)

    def _tail_events(self):
        evs = []
        for k, s in self.streams.items():
            last = None
            ndma = 0
            for ins in reversed(s):
                if ins.fn is None:
                    continue
                if ins.is_dma:
                    if ndma < NDMA_SEM:
                        evs.append(ins)
                        ndma += 1
                elif last is None:
                    last = ins
                    evs.append(ins)
                if last is not None and ndma >= NDMA_SEM:
                    break
        return evs

    def barrier(self, engs=("pe", "act", "dve", "pool", "sp")):
        evs = self._tail_events()
        for k in engs:
            ins = Instr(k, None, False)
            ins.waits = [d for d in evs if not (d.eng == k and not d.is_dma)]
            ins.idx = len(self.streams[k])
            self.streams[k].append(ins)

    def finish(self):
        self.barrier(engs=("sp",))

    def emit(self):
        nc = self.nc
        for s in self.streams.values():
            for ins in s:
                for d in ins.waits:
                    d.signal = True
        sems = {}
        import contextlib
        stack = contextlib.ExitStack()
        with stack:
            EPOCH = 16000
            nsig = {k: sum(1 for i_ in self.streams[k] if (not i_.is_dma) and i_.signal) for k in self.COMPUTE}
            for k in self.COMPUTE:
                sems[k] = [stack.enter_context(nc.semaphore("s_%s_%d" % (k, j))) for j in range(nsig[k] // EPOCH + 1)]
            dsems = {}
            for q in ("sp", "pool", "act"):
                if self.dma_count[q]:
                    dsems[q] = [stack.enter_context(nc.semaphore("d_%s_%d" % (q, i))) for i in range(NDMA_SEM)]
            for k in self.COMPUTE:
                c = 0
                for ins in self.streams[k]:
                    if ins.is_dma:
                        continue
                    if ins.signal:
                        ins.semval = c
                        c += 1
            block = stack.enter_context(nc.Block())

            def mk(engname):
                stream = self.streams[engname]

                def body(e):
                    waited = {}
                    dma_seen = 0
                    for ins in stream:
                        need = {}
                        for d in ins.waits:
                            if d.is_dma:
                                sem = dsems[d.eng][d.dma_slot % NDMA_SEM]
                                val = 16 * (d.dma_slot // NDMA_SEM + 1)
                            else:
                                sem = sems[d.eng][d.semval // EPOCH]
                                val = d.semval % EPOCH + 1
                            key = id(sem)
                            if need.get(key, (None, 0))[1] < val:
                                need[key] = (sem, val)
                        if ins.is_dma:
                            n = ins.dma_slot
                            if n >= NDMA_SEM:
                                sem = dsems[engname][n % NDMA_SEM]
                                val = 16 * (n // NDMA_SEM)
                                key = id(sem)
                                if need.get(key, (None, 0))[1] < val:
                                    need[key] = (sem, val)
                        for key, (sem, val) in need.items():
                            if waited.get(key, 0) >= val:
                                continue
                            waited[key] = val
                            e.wait_ge(sem, val)
                        if ins.fn is None:
                            continue
                        r = ins.fn(e)
                        if ins.is_dma:
                            r.then_inc(dsems[engname][ins.dma_slot % NDMA_SEM], 16)
                        elif ins.signal:
                            r.then_inc(sems[engname][ins.semval // EPOCH], 1)
                return body

            for engname, deco in (("sp", block.sync), ("pe", block.tensor), ("act", block.scalar),
                                  ("dve", block.vector), ("pool", block.gpsimd)):
                if self.streams[engname]:
                    deco(mk(engname))


D = 1024
NG = 64
GC = 16
NS = 64
T = 8
NH = 8
HD = 64
VD = 128
FF = 2816
F2 = 2 * FF
PAGE = 128
DEC_B = 128
DEC_S = 8
NCORE = 8
SPC = DEC_B // NCORE
PROMPT_CORES = (0, 4)
EPS = 1e-6


class Cfg:
    def __init__(self, seq=8192, past_len=2048, n_pool=2560):
        self.seq = seq
        self.past_len = past_len
        self.n_pages = past_len // PAGE
        self.n_pool = n_pool
        self.nblk = seq // 128


WEIGHT_NAMES = ['a_pre_g', 'a_post_g', 'ssm_lam_re', 'ssm_lam_im', 'ssm_log_dt', 'ssm_b_re', 'ssm_b_im',
                'ssm_c_re', 'ssm_c_im', 'ssm_d', 'glu_w', 'kv_norm_g', 'w_k', 'w_v', 'b_pre_g', 'b_post_g',
                'w_q', 'lam_q1', 'lam_k1', 'lam_q2', 'lam_k2', 'sub_g', 'w_o', 'f_pre_g', 'f_post_g',
                'w_up', 'conv_w', 'conv_b', 'w_down']


def make_in_maps(cfg, inputs):
    f = lambda a: np.ascontiguousarray(np.asarray(a))
    xp = f(inputs['x_prompt'])
    zero_prompt = np.zeros((cfg.seq, D), np.float32)
    weights = {k: f(inputs[k]) for k in WEIGHT_NAMES}
    ck = f(inputs['cache_k']).reshape(cfg.n_pool * PAGE, NH * 2 * HD)
    cv = f(inputs['cache_v']).reshape(cfg.n_pool * PAGE, NH * VD)
    maps = []
    for c in range(NCORE):
        sl = slice(c * SPC, (c + 1) * SPC)
        m = dict(weights)
        m['xp'] = xp[PROMPT_CORES.index(c)] if c in PROMPT_CORES else zero_prompt
        m['xs'] = f(inputs['x_sample'][sl]).reshape(SPC * DEC_S, D)
        m['st_re'] = f(inputs['state_ssm_re'][0, sl])
        m['st_im'] = f(inputs['state_ssm_im'][0, sl])
        m['st_conv'] = f(inputs['state_conv'][:, sl])
        m['cache_k'] = ck
        m['cache_v'] = cv
        m['page_table'] = f(inputs['page_table'][sl])
        maps.append(m)
    return maps


PI = float(np.pi)


def host_consts():
    ident = np.eye(128, dtype=np.float32)
    jj = np.arange(128) // GC
    cmask8 = (jj[None, :] >= jj[:, None]).astype(np.float32)
    pp = np.arange(128)
    qq = np.arange(512)
    tri = np.stack([(d_ * 128 + pp[:, None] <= qq[None, :]) for d_ in range(4)], axis=1).astype(np.float32)
    rel = (pp[:, None] - 127 - 128 * np.arange(64)[None, :]).astype(np.float32)
    iota = pp.astype(np.float32)[:, None]
    return {'c_ident': ident, 'c_cmask8': cmask8, 'c_tri': tri, 'c_rel': rel, 'c_iota': iota}


class Arena:
    def __init__(self, nc, nbytes=206 * 1024):
        self.nc, self.off, self.limit = nc, 0, nbytes
        self.base = nc.alloc_sbuf_tensor("arena", [128, nbytes // 4], F32)
        self.views = {F32: self.base, BF16: self.base.bitcast(BF16), I32: self.base.bitcast(I32)}

    def mark(self):
        return self.off

    def reset(self, m):
        self.off = m

    def alloc(self, shape, dtype, name=None):
        esz = 2 if dtype == BF16 else 4
        n = int(np.prod(shape[1:]))
        off = (self.off + 63) // 64 * 64
        assert off + n * esz <= self.limit, ("SBUF arena overflow", name, off, n * esz)
        self.off = off + n * esz
        ap = self.views[dtype][0:shape[0], off // esz: off // esz + n]
        if len(shape) == 3:
            ap = ap.rearrange("p (a b) -> p a b", a=shape[1])
        elif len(shape) == 4:
            ap = ap.rearrange("p (a b c) -> p a b c", a=shape[1], b=shape[2])
        return ap


class Ops:
    def __init__(self, P):
        self.P = P

    def tt(self, eng, out, a, b, op, r, w):
        return self.P.op(eng, lambda e: e.tensor_tensor(out=out, in0=a, in1=b, op=op), reads=r, writes=w)

    def ts(self, eng, out, a, s1, op0, r, w, s2=None, op1=None):
        if op1 is None:
            return self.P.op(eng, lambda e: e.tensor_scalar(out=out, in0=a, scalar1=s1, scalar2=None, op0=op0), reads=r, writes=w)
        return self.P.op(eng, lambda e: e.tensor_scalar(out=out, in0=a, scalar1=s1, scalar2=s2, op0=op0, op1=op1), reads=r, writes=w)

    def act(self, out, a, func, r, w, scale=1.0, bias=None):
        if bias is None:
            return self.P.op("act", lambda e: e.activation(out=out, in_=a, func=func, scale=scale), reads=r, writes=w)
        return self.P.op("act", lambda e: e.activation(out=out, in_=a, func=func, scale=scale, bias=bias), reads=r, writes=w)

    def copy(self, eng, out, a, r, w):
        if eng == "act":
            return self.P.op(eng, lambda e: e.activation(out=out, in_=a, func=AF.Copy), reads=r, writes=w)
        return self.P.op(eng, lambda e: e.tensor_copy(out=out, in_=a), reads=r, writes=w)

    def mm(self, out, lhsT, rhs, r, w, start=True, stop=True):
        return self.P.op("pe", lambda e: e.matmul(out, lhsT, rhs, start=start, stop=stop), reads=r, writes=w)

    def tr(self, out, in_, ident, r, w):
        return self.P.op("pe", lambda e: e.transpose(out, in_, ident), reads=r, writes=w)


def s5_params(P, O, nc, A, W, CT, ps):
    B = Buf("s5par")
    R, Wr = [B], [B]

    def T_(shape, dt=F32):
        return A.alloc(shape, dt)
    Vr, Vn = T_([64, NG, T * GC], BF16), T_([64, NG, T * GC], BF16)
    Wsr, Wsi = T_([128, NG, NS], BF16), T_([128, NG, NS], BF16)
    Kf = T_([128, NG, T * GC], BF16)
    idt = T_([128, 128])
    pers = {n: (T_([64, NG]), T_([64, NG])) for n in (T, 128)}
    keep = A.mark()
    lre, lim, dtb = T_([64, NG]), T_([64, NG]), T_([64, NG])
    P.dma("sp", lre[:], W['ssm_lam_re'][0].rearrange("g p -> p g"), writes=Wr, allow_slow_non_contiguous=True)
    P.dma("sp", lim[:], W['ssm_lam_im'][0].rearrange("g p -> p g"), writes=Wr, allow_slow_non_contiguous=True)
    P.dma("sp", dtb[:], W['ssm_log_dt'][0, :].partition_broadcast(64), writes=Wr)
    Bre, Bim = T_([64, NG, GC]), T_([64, NG, GC])
    Cre, Cim = T_([64, NG, GC]), T_([64, NG, GC])
    P.dma("sp", Bre[:], W['ssm_b_re'][0].rearrange("g p c -> p g c"), writes=Wr)
    P.dma("sp", Bim[:], W['ssm_b_im'][0].rearrange("g p c -> p g c"), writes=Wr)
    P.dma("sp", Cre[:], W['ssm_c_re'][0].rearrange("g c p -> p g c"), writes=Wr, allow_slow_non_contiguous=True)
    P.dma("sp", Cim[:], W['ssm_c_im'][0].rearrange("g c p -> p g c"), writes=Wr, allow_slow_non_contiguous=True)
    O.act(dtb[:], dtb[:], AF.Exp, R, Wr)
    a_, th = T_([64, NG]), T_([64, NG])
    O.tt("dve", a_[:], lre[:], dtb[:], ALU.mult, R, Wr)
    O.tt("dve", th[:], lim[:], dtb[:], ALU.mult, R, Wr)

    tmp, tmp2, er = T_([64, NG]), T_([64, NG]), T_([64, NG])

    MAGIC = 12582912.0
    C1 = 6.28125
    C2 = 2 * PI - 6.28125
    kq = T_([64, NG])

    def reduce_arg(dst, n, shift):
        O.ts("dve", dst[:], th[:], float(n), ALU.mult, R, Wr, s2=shift, op1=ALU.add)
        O.ts("dve", kq[:], dst[:], 1.0 / (2 * PI), ALU.mult, R, Wr, s2=MAGIC, op1=ALU.add)
        O.ts("dve", kq[:], kq[:], -MAGIC, ALU.add, R, Wr)
        P.op("dve", lambda e: e.scalar_tensor_tensor(out=dst[:], in0=kq[:], scalar=-C1, in1=dst[:],
                                                     op0=ALU.mult, op1=ALU.add), reads=R, writes=Wr)
        P.op("dve", lambda e: e.scalar_tensor_tensor(out=dst[:], in0=kq[:], scalar=-C2, in1=dst[:],
                                                     op0=ALU.mult, op1=ALU.add), reads=R, writes=Wr)

    def power(n):
        pr, pi_ = pers[n] if n in pers else (T_([64, NG]), T_([64, NG]))
        O.act(er[:], a_[:], AF.Exp, R, Wr, scale=float(n))
        reduce_arg(tmp, n, 0.0)
        O.act(pi_[:], tmp[:], AF.Sin, R, Wr)
        reduce_arg(tmp2, n, 0.5 * PI)
        O.act(pr[:], tmp2[:], AF.Sin, R, Wr)
        O.tt("dve", pr[:], pr[:], er[:], ALU.mult, R, Wr)
        O.tt("dve", pi_[:], pi_[:], er[:], ALU.mult, R, Wr)
        return pr, pi_

    pw = {n: power(n) for n in list(range(-T, T + 1)) + [128]}
    l1r, l1i = pw[1]
    nr, den, qr, qi = T_([64, NG]), T_([64, NG]), T_([64, NG]), T_([64, NG])
    O.ts("dve", nr[:], l1r[:], -1.0, ALU.add, R, Wr)
    O.tt("dve", den[:], lre[:], lre[:], ALU.mult, R, Wr)
    O.tt("dve", tmp[:], lim[:], lim[:], ALU.mult, R, Wr)
    O.tt("dve", den[:], den[:], tmp[:], ALU.add, R, Wr)
    P.op("dve", lambda e: e.reciprocal(out=den[:], in_=den[:]), reads=R, writes=Wr)
    O.tt("dve", qr[:], nr[:], lre[:], ALU.mult, R, Wr)
    O.tt("dve", tmp[:], l1i[:], lim[:], ALU.mult, R, Wr)
    O.tt("dve", qr[:], qr[:], tmp[:], ALU.add, R, Wr)
    O.tt("dve", qr[:], qr[:], den[:], ALU.mult, R, Wr)
    O.tt("dve", qi[:], l1i[:], lre[:], ALU.mult, R, Wr)
    O.tt("dve", tmp[:], nr[:], lim[:], ALU.mult, R, Wr)
    O.tt("dve", qi[:], qi[:], tmp[:], ALU.subtract, R, Wr)
    O.tt("dve", qi[:], qi[:], den[:], ALU.mult, R, Wr)

    def bc(t_):
        return t_[:].unsqueeze(2).to_broadcast([64, NG, GC])

    def cmul(or_, oi_, ar, ai, br, bi, t1):
        O.tt("dve", or_, ar, br, ALU.mult, R, Wr)
        O.tt("dve", t1, ai, bi, ALU.mult, R, Wr)
        O.tt("dve", or_, or_, t1, ALU.subtract, R, Wr)
        O.tt("dve", oi_, ar, bi, ALU.mult, R, Wr)
        O.tt("dve", t1, ai, br, ALU.mult, R, Wr)
        O.tt("dve", oi_, oi_, t1, ALU.add, R, Wr)

    t3 = T_([64, NG, GC])
    Bbr, Bbi = T_([64, NG, GC]), T_([64, NG, GC])
    cmul(Bbr[:], Bbi[:], bc(qr), bc(qi), Bre[:], Bim[:], t3[:])
    Sr, Si = T_([64, NG, T, GC]), T_([64, NG, T, GC])
    Xr, Xi = T_([64, NG, T * GC], BF16), T_([64, NG, T * GC], BF16)
    flat = lambda t_: t_[:].rearrange("p g j c -> p g (j c)")
    for j in range(T):
        pr, pi_ = pw[j + 1]
        cmul(Sr[:, :, j, :], Si[:, :, j, :], bc(pr), bc(pi_), Cre[:], Cim[:], t3[:])
    O.copy("dve", Vr[:], flat(Sr), R, Wr)
    O.ts("dve", Vn[:], flat(Si), -1.0, ALU.mult, R, Wr)
    for j in range(T):
        pr, pi_ = pw[-(j + 1)]
        cmul(Sr[:, :, j, :], Si[:, :, j, :], bc(pr), bc(pi_), Bbr[:], Bbi[:], t3[:])
    O.copy("dve", Xr[:], flat(Sr), R, Wr)
    O.copy("dve", Xi[:], flat(Si), R, Wr)
    cm = T_([128, 128])
    P.dma("sp", cm[:], CT['c_cmask8'][:, :], writes=Wr)
    for g in range(NG):
        pt, pb = ps[g % 2]
        O.mm(pt[:, 0:128], Xr[:, g, :], Vr[:, g, :], R, [pb], start=True, stop=False)
        O.mm(pt[:, 0:128], Xi[:, g, :], Vn[:, g, :], R, [pb], start=False, stop=True)
        O.tt("dve", Kf[:, g, :], pt[:, 0:128], cm[:], ALU.mult, R + [pb], Wr + [pb])
    for j in range(T):
        pr, pi_ = pw[T - 1 - j]
        cmul(Sr[:, :, j, :], Si[:, :, j, :], bc(pr), bc(pi_), Bbr[:], Bbi[:], t3[:])
    P.dma("sp", idt[:], CT['c_ident'][:, :], writes=Wr)
    for g in range(NG):
        for k, (src, dst) in enumerate(((Sr, Wsr), (Si, Wsi))):
            pt, pb = ps[(2 * g + k) % 2]
            O.tr(pt[:, 0:64], src[:, g, :, :].rearrange("p j c -> p (j c)"), idt[0:64, 0:64], R, [pb])
            O.copy("act", dst[:, g, :], pt[:, 0:64], R + [pb], Wr + [pb])
    return dict(Vr=Vr, Vn=Vn, Wsr=Wsr, Wsi=Wsi, Kf=Kf, a8=pw[T], a128=pw[128], ident=idt, B=B, keep=keep)


def build_program(cfg, debug=False, stop_after=None):
    nc = bass.Bass("TRN2", target_bir_lowering=False)
    SEQ = cfg.seq
    NBLK = cfg.nblk
    NTOK = SEQ + 128
    NPG = cfg.n_pages
    NKS = NPG * PAGE + DEC_S
    din = lambda n, shp, dt=F32: nc.dram_tensor(n, list(shp), dt, kind="ExternalInput").ap()
    dout = lambda n, shp, dt=F32: nc.dram_tensor(n, list(shp), dt, kind="ExternalOutput").ap()
    dint = lambda n, shp, dt=F32: nc.dram_tensor(n, list(shp), dt, kind="Internal").ap()
    wshape = {'a_pre_g': [1, D], 'a_post_g': [1, D], 'ssm_lam_re': [1, NG, NS], 'ssm_lam_im': [1, NG, NS],
              'ssm_log_dt': [1, NG], 'ssm_b_re': [1, NG, NS, GC], 'ssm_b_im': [1, NG, NS, GC],
              'ssm_c_re': [1, NG, GC, NS], 'ssm_c_im': [1, NG, GC, NS], 'ssm_d': [1, D], 'glu_w': [1, D, 2 * D],
              'kv_norm_g': [D], 'w_k': [D, D], 'w_v': [D, D], 'b_pre_g': [1, D], 'b_post_g': [1, D],
              'w_q': [1, D, D], 'lam_q1': [1, HD], 'lam_k1': [1, HD], 'lam_q2': [1, HD], 'lam_k2': [1, HD],
              'sub_g': [1, VD], 'w_o': [1, D, D], 'f_pre_g': [2, D], 'f_post_g': [2, D], 'w_up': [2, D, F2],
              'conv_w': [2, 3, F2], 'conv_b': [2, F2], 'w_down': [2, FF, D]}
    W = {k: din(k, v) for k, v in wshape.items()}
    CT = {k: din(k, v.shape) for k, v in host_consts().items()}
    xp = din('xp', [SEQ, D]); xs = din('xs', [128, D])
    st_re = din('st_re', [SPC, NG, NS]); st_im = din('st_im', [SPC, NG, NS])
    st_conv = din('st_conv', [2, SPC, 2, F2])
    cache_k = din('cache_k', [cfg.n_pool * PAGE, D]); cache_v = din('cache_v', [cfg.n_pool * PAGE, D])
    page_table = din('page_table', [SPC, NPG], I32)
    o_yp = dout('o_yp', [SEQ, D]); o_ys = dout('o_ys', [128, D])
    o_srp = dout('o_srp', [NG, NS]); o_sip = dout('o_sip', [NG, NS])
    o_cp = dout('o_cp', [2, 2, F2])
    o_kp = dout('o_kp', [SEQ, D]); o_vp = dout('o_vp', [SEQ, D])
    o_srs = dout('o_srs', [SPC, NG, NS]); o_sis = dout('o_sis', [SPC, NG, NS])
    o_cs = dout('o_cs', [2, SPC, 2, F2])
    o_ks = dout('o_ks', [128, D]); o_vs = dout('o_vs', [128, D])
    xres = dint('xres', [NTOK, D])
    zs5 = dint('zs5', [NTOK, D], BF16)
    KTd = dint('KTd', [NH, 128, NTOK], BF16)
    QTd = dint('QTd', [NH, 128, NTOK], BF16)
    OTd = dint('OTd', [NH, 128, NTOK], BF16)
    Vbd = dint('Vbd', [NTOK, D], BF16)
    dbg = {}
    if debug:
        dbg['x1'] = dout('d_x1', [NTOK, D]); dbg['x2'] = dout('d_x2', [NTOK, D]); dbg['x3'] = dout('d_x3', [NTOK, D])

    P = Prog(nc); O = Ops(P); A = Arena(nc)
    st = contextlib.ExitStack()
    with st:
        ps = [(st.enter_context(nc.psum_tensor("ps%d" % i, [128, 512], F32)), Buf()) for i in range(6)]
        psb = [(st.enter_context(nc.psum_tensor("psb%d" % i, [128, 1024], BF16)), Buf()) for i in range(2)]
        ctx = dict(nc=nc, P=P, O=O, A=A, W=W, CT=CT, ps=ps, psb=psb, cfg=cfg, SEQ=SEQ, NBLK=NBLK, NTOK=NTOK,
                   xp=xp, xs=xs, xres=xres, zs5=zs5, KTd=KTd, QTd=QTd, OTd=OTd, Vbd=Vbd, dbg=dbg,
                   st_re=st_re, st_im=st_im, st_conv=st_conv, cache_k=cache_k, cache_v=cache_v,
                   page_table=page_table, NPG=NPG, NKS=NKS,
                   o=dict(yp=o_yp, ys=o_ys, srp=o_srp, sip=o_sip, cp=o_cp, kp=o_kp, vp=o_vp, srs=o_srs,
                          sis=o_sis, cs=o_cs, ks=o_ks, vs=o_vs))
        phases = [("s5", lambda: phase_s5(ctx)), ("glu", lambda: phase_glu(ctx)),
                  ("ffn0", lambda: phase_ffn(ctx, 0, xres[0:SEQ], xres[SEQ:NTOK], 'x2')),
                  ("kvq", lambda: phase_kvq(ctx)), ("attn", lambda: phase_attn(ctx)),
                  ("oproj", lambda: phase_oproj(ctx)),
                  ("ffn1", lambda: phase_ffn(ctx, 1, o_yp, o_ys, None))]
        for name, fn in phases:
            fn()
            P.barrier(); A.reset(0)
            if stop_after == name:
                break
        P.finish()
        P.emit()
    return nc


def rms_rstd(ctx, xt, sq, ss, n, bx, bs):
    P, O = ctx['P'], ctx['O']
    O.tt("dve", sq, xt, xt, ALU.mult, [bx], [bs])
    P.op("dve", lambda e: e.reduce_sum(out=ss, in_=sq, axis=AX.X), reads=[bs], writes=[bs])
    O.ts("dve", ss, ss, 1.0 / xt.shape[-1], ALU.mult, [bs], [bs], s2=EPS, op1=ALU.add)
    O.act(ss, ss, AF.Sqrt, [bs], [bs])
    P.op("dve", lambda e: e.reciprocal(out=ss, in_=ss), reads=[bs], writes=[bs])


def gelu_tanh(ctx, out, x, t1, r, w):
    P, O = ctx['P'], ctx['O']
    O.tt("dve", t1, x, x, ALU.mult, r, w)
    O.ts("dve", t1, t1, 0.044715, ALU.mult, w, w, s2=1.0, op1=ALU.add)
    O.tt("dve", t1, t1, x, ALU.mult, r + w, w)
    O.act(t1, t1, AF.Sigmoid, w, w, scale=1.5957691216057308)
    O.tt("dve", out, t1, x, ALU.mult, r + w, w)


def load_bcast(ctx, vec_ap, n=128, dt=F32):
    t = ctx['A'].alloc([n, vec_ap.shape[-1]], dt)
    b = Buf()
    ctx['P'].dma("sp", t, vec_ap.partition_broadcast(n), writes=[b])
    return t, b


def load_w_bf16(ctx, w_ap, kchunks, ncols):
    t = ctx['A'].alloc([128, kchunks, ncols], BF16)
    b = Buf()
    src = w_ap.rearrange("(k p) n -> p k n", p=128)
    step = max(1, 4096 // ncols)
    for k0 in range(0, kchunks, step):
        k1 = min(kchunks, k0 + step)
        ctx['P'].dma("pool", t[:, k0:k1, :], src[:, k0:k1, :], writes=[b])
    return t, b


def transpose_block(ctx, src_bf, idb, dstT, rb, wb, nk=8, n=128):
    P, O = ctx['P'], ctx['O']
    pt, pb = ctx['psb'][ctx.setdefault('_tb', 0) % 2]
    ctx['_tb'] += 1
    for k in range(nk):
        O.tr(pt[:, k * 128:k * 128 + n], src_bf[0:n, k * 128:(k + 1) * 128], idb[0:n, 0:n], rb, [pb])
    O.copy("act", dstT[:, :, 0:n], pt[:, 0:nk * 128].rearrange("p (k t) -> p k t", k=nk)[:, :, 0:n], [pb], wb + [pb])


def phase_s5(ctx):
    P, O, A, W, CT, ps, psb = (ctx[k] for k in ('P', 'O', 'A', 'W', 'CT', 'ps', 'psb'))
    nc = ctx['nc']
    SEQ, NTOK = ctx['SEQ'], ctx['NTOK']
    tb = s5_params(P, O, nc, A, W, CT, ps[0:2])
    P.barrier(); A.reset(tb['keep'])
    Bt = tb['B']; R = [Bt]
    gb, bG = load_bcast(ctx, W['a_pre_g'][0])
    gd, bD = load_bcast(ctx, W['ssm_d'][0])
    O.tt("dve", gd, gd, gb, ALU.mult, [bG, bD], [bD])
    idb = A.alloc([128, 128], BF16)
    O.copy("dve", idb, tb['ident'], R, R)
    a8r, a8i = tb['a8']
    CS = 64
    UT = A.alloc([128, NG, CS], BF16)
    Er, Ei = A.alloc([64, NG, CS], F32), A.alloc([64, NG, CS], F32)
    sA = [A.alloc([64, NG], F32) for _ in range(4)]
    t1, t2, t3, t4 = (A.alloc([64, NG], F32) for _ in range(4))
    bS = [Buf(), Buf()]
    bT = Buf(); bT2 = Buf()
    hbuf = [(A.alloc([64, CS], BF16), A.alloc([64, CS], BF16), Buf()) for _ in range(2)]
    P.op("dve", lambda e: e.memset(sA[0], 0.0), writes=[bS[0]])
    P.op("dve", lambda e: e.memset(sA[1], 0.0), writes=[bS[0]])
    cur = 0
    xt = A.alloc([128, T, D], F32)
    yt = A.alloc([128, T, D], F32)
    ub = A.alloc([128, NG, T, GC], BF16)
    ss = A.alloc([128, T], F32)
    bx, bs, bu, by, bUT, bE = Buf(), Buf(), Buf(), Buf(), Buf(), Buf()
    segs = [(xv0, min(CS, SEQ // T - xv0), False) for xv0 in range(0, SEQ // T, CS)] + [(0, SPC, True)]
    xpv = ctx['xp'].rearrange("(ch i) d -> ch i d", i=T)
    xsv = ctx['xs'].rearrange("(ch i) d -> ch i d", i=T)
    for (c0, cn, is_s) in segs:
        src = xsv[0:cn] if is_s else xpv[c0:c0 + cn]
        row0 = SEQ if is_s else c0 * T
        P.dma("sp", xt[0:cn], src, writes=[bx])
        P.dma("sp", ctx['xres'][row0:row0 + cn * T, :].rearrange("(ch i) d -> ch i d", i=T), xt[0:cn], reads=[bx])
        xf = xt[0:cn].rearrange("p i d -> p (i d)")
        yf = yt[0:cn].rearrange("p i d -> p (i d)")
        O.tt("dve", yf, xf, xf, ALU.mult, [bx], [by])
        P.op("dve", lambda e, cn=cn: e.reduce_sum(out=ss[0:cn], in_=yt[0:cn], axis=AX.X), reads=[by], writes=[bs])
        O.ts("dve", ss[0:cn], ss[0:cn], 1.0 / D, ALU.mult, [bs], [bs], s2=EPS, op1=ALU.add)
        O.act(ss[0:cn], ss[0:cn], AF.Sqrt, [bs], [bs])
        P.op("dve", lambda e, cn=cn: e.reciprocal(out=ss[0:cn], in_=ss[0:cn]), reads=[bs], writes=[bs])
        for i in range(T):
            P.op("dve", lambda e, i=i, cn=cn: e.scalar_tensor_tensor(
                out=ub[0:cn, :, i, :], in0=xt[0:cn, i, :].rearrange("p (g c) -> p g c", c=GC),
                scalar=ss[0:cn, i:i + 1], in1=gb[0:cn, :].rearrange("p (g c) -> p g c", c=GC),
                op0=ALU.mult, op1=ALU.mult), reads=[bx, bs, bG], writes=[bu])
            P.op("dve", lambda e, i=i, cn=cn: e.scalar_tensor_tensor(
                out=yt[0:cn, i, :], in0=xt[0:cn, i, :], scalar=ss[0:cn, i:i + 1], in1=gd[0:cn, :],
                op0=ALU.mult, op1=ALU.mult), reads=[bx, bs, bD], writes=[by])
        for g in range(NG):
            pt, pb = psb[g % 2]
            O.tr(pt[:, 0:cn], ub[0:cn, g, :, :].rearrange("p j c -> p (j c)"), idb[0:cn, 0:cn], [bu] + R, [pb])
            O.copy("act", UT[:, g, 0:cn], pt[:, 0:cn], [pb], [pb, bUT])
        for g in range(NG):
            for k, (Wt, Et) in enumerate(((tb['Wsr'], Er), (tb['Wsi'], Ei))):
                pt, pb = ps[(2 * g + k) % 2]
                O.mm(pt[0:64, 0:cn], Wt[:, g, :], UT[:, g, 0:cn], [bUT] + R, [pb])
                O.copy("act", Et[:, g, 0:cn], pt[0:64, 0:cn], [pb], [pb, bE])
        if is_s:
            xflat = xt.rearrange("p i d -> p (i d)")
            carve = lambda n_: xflat[0:64, n_ * 1024:(n_ + 1) * 1024].rearrange("p (g s) -> p g s", s=SPC)
            Hr, Hi, Fr, Fi, u1, u2 = (carve(n_) for n_ in range(6))
            bH = bF = bx
            for s_ in range(SPC):
                P.dma("sp", Hr[:, :, s_], ctx['st_re'][s_].rearrange("g p -> p g"), reads=[bu, by], writes=[bH], allow_slow_non_contiguous=True)
                P.dma("sp", Hi[:, :, s_], ctx['st_im'][s_].rearrange("g p -> p g"), reads=[bu, by], writes=[bH], allow_slow_non_contiguous=True)
            ar = a8r.unsqueeze(2).to_broadcast([64, NG, SPC]); ai = a8i.unsqueeze(2).to_broadcast([64, NG, SPC])
            O.tt("dve", u1, Hr, ar, ALU.mult, [bH] + R, [bF]); O.tt("dve", u2, Hi, ai, ALU.mult, [bH] + R, [bF])
            O.tt("dve", u1, u1, u2, ALU.subtract, [bF], [bF]); O.tt("dve", Fr, u1, Er[:, :, 0:cn], ALU.add, [bF, bE], [bF])
            O.tt("dve", u1, Hi, ar, ALU.mult, [bH] + R, [bF]); O.tt("dve", u2, Hr, ai, ALU.mult, [bH] + R, [bF])
            O.tt("dve", u1, u1, u2, ALU.add, [bF], [bF]); O.tt("dve", Fi, u1, Ei[:, :, 0:cn], ALU.add, [bF, bE], [bF])
            for s_ in range(SPC):
                P.dma("sp", ctx['o']['srs'][s_].rearrange("g p -> p g"), Fr[:, :, s_], reads=[bF], allow_slow_non_contiguous=True)
                P.dma("sp", ctx['o']['sis'][s_].rearrange("g p -> p g"), Fi[:, :, s_], reads=[bF], allow_slow_non_contiguous=True)
            Hsr, Hsi, bHH = Hr, Hi, bH
        else:
            for k in range(cn):
                s_r, s_i = sA[2 * cur], sA[2 * cur + 1]
                n_r, n_i = sA[2 * (1 - cur)], sA[2 * (1 - cur) + 1]
                bc_, bn_ = bS[cur], bS[1 - cur]
                O.tt("dve", t1, s_r, a8r, ALU.mult, [bc_] + R, [bT]); O.tt("dve", t2, s_i, a8i, ALU.mult, [bc_] + R, [bT])
                O.tt("dve", t1, t1, t2, ALU.subtract, [bT], [bT]); O.tt("dve", n_r, t1, Er[:, :, k], ALU.add, [bT, bE], [bn_])
                O.tt("pool", t3, s_i, a8r, ALU.mult, [bc_] + R, [bT2]); O.tt("pool", t4, s_r, a8i, ALU.mult, [bc_] + R, [bT2])
                O.tt("pool", t3, t3, t4, ALU.add, [bT2], [bT2]); O.tt("pool", n_i, t3, Ei[:, :, k], ALU.add, [bT2, bE], [bn_])
                O.copy("act", Er[:, :, k], s_r, [bc_, bn_], [bE])
                O.copy("act", Ei[:, :, k], s_i, [bc_, bn_], [bE])
                cur = 1 - cur
            Hsr, Hsi, bHH = Er, Ei, bE
        for g in range(NG):
            pt, pb = ps[2 + g % 4]
            hr_b, hi_b, bhb = hbuf[g % 2]
            O.copy("pool", hr_b[:, 0:cn], Hsr[:, g, 0:cn], [bHH], [bhb])
            O.copy("pool", hi_b[:, 0:cn], Hsi[:, g, 0:cn], [bHH], [bhb])
            O.mm(pt[0:cn, 0:128], UT[:, g, 0:cn], tb['Kf'][:, g, :], [bUT] + R, [pb], start=True, stop=False)
            O.mm(pt[0:cn, 0:128], hr_b[:, 0:cn], tb['Vr'][:, g, :], [bhb] + R, [pb], start=False, stop=False)
            O.mm(pt[0:cn, 0:128], hi_b[:, 0:cn], tb['Vn'][:, g, :], [bhb] + R, [pb], start=False, stop=True)
            O.tt("dve", yt[0:cn, :, g * GC:(g + 1) * GC], yt[0:cn, :, g * GC:(g + 1) * GC],
                 pt[0:cn, 0:128].rearrange("p (i c) -> p i c", c=GC), ALU.add, [by, pb], [by, pb])
        zb = ub[0:cn].rearrange("p g j c -> p (g j c)")
        gelu_tanh(ctx, zb, yf, xf, [by, bUT], [bx, bu])
        P.dma("sp", ctx['zs5'][row0:row0 + cn * T, :].rearrange("(ch i) d -> ch (i d)", i=T), zb, reads=[bu])
    P.dma("sp", ctx['o']['srp'].rearrange("g p -> p g"), sA[2 * cur], reads=[bS[cur]], allow_slow_non_contiguous=True)
    P.dma("sp", ctx['o']['sip'].rearrange("g p -> p g"), sA[2 * cur + 1], reads=[bS[cur]], allow_slow_non_contiguous=True)


def token_blocks(ctx):
    return [(b * 128, False) for b in range(ctx['NBLK'])] + [(ctx['SEQ'], True)]


def phase_glu(ctx):
    P, O, A, W, CT, ps, psb = (ctx[k] for k in ('P', 'O', 'A', 'W', 'CT', 'ps', 'psb'))
    idf = A.alloc([128, 128], F32); idb = A.alloc([128, 128], BF16); bI = Buf()
    P.dma("sp", idf, CT['c_ident'], writes=[bI]); O.copy("dve", idb, idf, [bI], [bI])
    Wg, bW = load_w_bf16(ctx, W['glu_w'][0], 8, 2 * D)
    gp, bG = load_bcast(ctx, W['a_post_g'][0])
    NB = 2
    zb = [A.alloc([128, D], BF16) for _ in range(NB)]; bz = [Buf() for _ in range(NB)]
    zT = [A.alloc([128, 8, 128], BF16) for _ in range(NB)]; bzT = [Buf() for _ in range(NB)]
    xt = [A.alloc([128, D], F32) for _ in range(NB)]; bx = [Buf() for _ in range(NB)]
    sg = A.alloc([128, D], F32); og = A.alloc([128, D], F32); sq = A.alloc([128, D], F32); ss = A.alloc([128, 1], F32)
    bsg, bog, bsq = Buf(), Buf(), Buf()
    for n, (r0, is_s) in enumerate(token_blocks(ctx)):
        i = n % NB
        P.dma("sp", zb[i], ctx['zs5'][r0:r0 + 128, :], writes=[bz[i]])
        P.dma("sp", xt[i], ctx['xres'][r0:r0 + 128, :], writes=[bx[i]])
        transpose_block(ctx, zb[i], idb, zT[i], [bz[i], bI], [bzT[i]])
        for q in range(4):
            pt, pb = ps[q]
            for k in range(8):
                O.mm(pt[:, :], zT[i][:, k, :], Wg[:, k, q * 512:(q + 1) * 512], [bzT[i], bW], [pb], start=(k == 0), stop=(k == 7))
        for q in range(2):
            O.act(sg[:, q * 512:(q + 1) * 512], ps[2 + q][0][:, :], AF.Sigmoid, [ps[2 + q][1]], [bsg, ps[2 + q][1]])
        for q in range(2):
            O.tt("dve", og[:, q * 512:(q + 1) * 512], sg[:, q * 512:(q + 1) * 512], ps[q][0][:, :], ALU.mult,
                 [bsg, ps[q][1]], [bog, ps[q][1]])
        rms_rstd(ctx, og, sq, ss, 128, bog, bsq)
        P.op("dve", lambda e: e.scalar_tensor_tensor(out=og, in0=og, scalar=ss[:, 0:1], in1=gp, op0=ALU.mult, op1=ALU.mult),
             reads=[bog, bsq, bG], writes=[bog])
        O.tt("dve", xt[i], xt[i], og, ALU.add, [bx[i], bog], [bx[i]])
        P.dma("sp", ctx['xres'][r0:r0 + 128, :], xt[i], reads=[bx[i]])
        if 'x1' in ctx['dbg']:
            P.dma("sp", ctx['dbg']['x1'][r0:r0 + 128, :], xt[i], reads=[bx[i]])


def assemble(cfg, R):
    c0, c1 = PROMPT_CORES
    S = cfg.seq
    cat = lambda k: np.concatenate([np.asarray(R[c][k]) for c in range(NCORE)], axis=0)
    y_prompt = np.stack([R[c0]['o_yp'], R[c1]['o_yp']]).astype(np.float32)
    y_sample = cat('o_ys').reshape(DEC_B, DEC_S, D)
    srp = np.stack([R[c0]['o_srp'], R[c1]['o_srp']])[None]
    sip = np.stack([R[c0]['o_sip'], R[c1]['o_sip']])[None]
    conv_p = np.stack([R[c0]['o_cp'], R[c1]['o_cp']], axis=1)
    k_p = np.stack([R[c0]['o_kp'], R[c1]['o_kp']]).reshape(2, S, NH, 2 * HD)
    v_p = np.stack([R[c0]['o_vp'], R[c1]['o_vp']]).reshape(2, S, NH, VD)
    srs = cat('o_srs')[None]; sis = cat('o_sis')[None]
    conv_s = np.concatenate([np.asarray(R[c]['o_cs']) for c in range(NCORE)], axis=1)
    k_s = cat('o_ks').reshape(DEC_B, DEC_S, NH, 2 * HD)
    v_s = cat('o_vs').reshape(DEC_B, DEC_S, NH, VD)
    return tuple(np.ascontiguousarray(a, dtype=np.float32) for a in
                 (y_prompt, y_sample, srp, sip, conv_p, k_p, v_p, srs, sis, conv_s, k_s, v_s))


def phase_ffn(ctx, l, dst_p, dst_s, dbgname):
    P, O, A, W, CT, ps, psb = (ctx[k] for k in ('P', 'O', 'A', 'W', 'CT', 'ps', 'psb'))
    SEQ = ctx['SEQ']
    NC_ = F2 // 128
    NG_ = NC_ // 2
    idf = A.alloc([128, 128], F32); idb = A.alloc([128, 128], BF16); bI = Buf()
    P.dma("sp", idf, CT['c_ident'], writes=[bI]); O.copy("dve", idb, idf, [bI], [bI])
    Wu, bWu = load_w_bf16(ctx, W['w_up'][l], 8, F2)
    Wd, bWd = load_w_bf16(ctx, W['w_down'][l], NG_, D)
    gpre, bG1 = load_bcast(ctx, W['f_pre_g'][l])
    gpost, bG2 = load_bcast(ctx, W['f_post_g'][l])
    cw = A.alloc([128, NC_, 3], F32); cb = A.alloc([128, NC_], F32); bC = Buf()
    for r in range(3):
        P.dma("sp", cw[:, :, r], W['conv_w'][l, r].rearrange("(c p) -> p c", p=128), writes=[bC], allow_slow_non_contiguous=True)
    P.dma("sp", cb, W['conv_b'][l].rearrange("(c p) -> p c", p=128), writes=[bC], allow_slow_non_contiguous=True)
    halo_p = A.alloc([128, NC_, 1, 2], F32); halo_s = A.alloc([128, NC_, SPC, 2], F32)
    bHp = [Buf() for _ in range(NC_)]; bHs = [Buf() for _ in range(NC_)]
    P.op("dve", lambda e: e.memset(halo_p.rearrange("p c s r -> p (c s r)"), 0.0), writes=bHp)
    m_sc = A.mark()
    sc = A.alloc([2 * SPC, F2], F32); bsc = Buf()
    P.dma("sp", sc, ctx['st_conv'][l].rearrange("s r f -> (s r) f"), writes=[bsc])
    for c in range(NC_):
        pt, pb = ps[c % 2]
        O.tr(pt[:, 0:2 * SPC], sc[:, c * 128:(c + 1) * 128], idf[0:2 * SPC, 0:2 * SPC], [bsc, bI], [pb])
        O.copy("act", halo_s[:, c].rearrange("p s r -> p (s r)"), pt[:, 0:2 * SPC], [pb], [pb, bHs[c]])
    P.barrier(); A.reset(m_sc)
    TB = 256
    xt = A.alloc([128, TB // 128, D], F32); sq = A.alloc([128, D], F32); og = A.alloc([128, D], F32)
    xn = A.alloc([128, D], BF16); xT = A.alloc([128, 8, TB], BF16); ss = A.alloc([128, 2], F32)
    bx, bsq, bog, bxn, bxT = Buf(), Buf(), Buf(), Buf(), Buf()
    hb = [A.alloc([128, 2, TB + 2], F32) for _ in range(2)]; bhb = [Buf(), Buf()]
    hc = [A.alloc([128, 2, TB], F32) for _ in range(2)]; bhc = [Buf(), Buf()]
    t1 = A.alloc([128, TB], F32); bt1 = Buf()
    gT = A.alloc([128, NG_, TB], BF16); bgT = Buf()
    blocks = [(r0, False, TB) for r0 in range(0, SEQ, TB)] + [(SEQ, True, 128)]
    for n, (r0, is_s, ntok) in enumerate(blocks):
        nseq, L = (SPC, DEC_S) if is_s else (1, ntok)
        nsub = ntok // 128
        halo, bH = (halo_s, bHs) if is_s else (halo_p, bHp)
        for sub in range(nsub):
            P.dma("sp", xt[:, sub, :], ctx['xres'][r0 + sub * 128:r0 + (sub + 1) * 128, :], writes=[bx])
            rms_rstd(ctx, xt[:, sub, :], sq, ss[:, 0:1], 128, bx, bsq)
            P.op("dve", lambda e, sub=sub: e.scalar_tensor_tensor(out=xn, in0=xt[:, sub, :], scalar=ss[:, 0:1], in1=gpre,
                                                              op0=ALU.mult, op1=ALU.mult), reads=[bx, bsq, bG1], writes=[bxn])
            transpose_block(ctx, xn, idb, xT[:, :, sub * 128:(sub + 1) * 128], [bxn, bI], [bxT])
        for c in range(NG_):
            i2 = c % 2
            hv = hb[i2][:, :, 0:nseq * (L + 2)].rearrange("p j (s l) -> p j s l", l=L + 2)
            hcv = hc[i2][:, :, 0:ntok].rearrange("p j (s l) -> p j s l", l=L)
            for j, cc in enumerate((c, c + NG_)):
                pt, pb = ps[(2 * c + j) % 4]
                for k in range(8):
                    O.mm(pt[:, 0:ntok], Wu[:, k, cc * 128:(cc + 1) * 128], xT[:, k, 0:ntok], [bWu, bxT], [pb], start=(k == 0), stop=(k == 7))
                O.copy("pool", hv[:, j, :, 0:2], halo[:, cc], [bH[cc]], [bhb[i2]])
                O.copy("act", hv[:, j, :, 2:L + 2], pt[:, 0:ntok].rearrange("p (s l) -> p s l", l=L), [pb], [pb, bhb[i2]])
                O.copy("pool", halo[:, cc], hv[:, j, :, L:L + 2], [bhb[i2]], [bH[cc]])
                P.op("dve", lambda e, hv=hv, hcv=hcv, j=j, cc=cc, L=L: e.tensor_scalar(
                    out=hcv[:, j], in0=hv[:, j, :, 2:L + 2], scalar1=cw[:, cc, 2:3], scalar2=cb[:, cc:cc + 1],
                    op0=ALU.mult, op1=ALU.add), reads=[bhb[i2], bC], writes=[bhc[i2]])
                for tap in (1, 0):
                    P.op("dve", lambda e, hv=hv, hcv=hcv, j=j, cc=cc, L=L, tap=tap: e.scalar_tensor_tensor(
                        out=hcv[:, j], in0=hv[:, j, :, tap:tap + L], scalar=cw[:, cc, tap:tap + 1], in1=hcv[:, j],
                        op0=ALU.mult, op1=ALU.add), reads=[bhb[i2], bC, bhc[i2]], writes=[bhc[i2]])
            gelu_tanh(ctx, t1[:, 0:ntok], hc[i2][:, 0, 0:ntok], t1[:, 0:ntok], [bhc[i2]], [bt1])
            O.tt("dve", gT[:, c, 0:ntok], t1[:, 0:ntok], hc[i2][:, 1, 0:ntok], ALU.mult, [bt1, bhc[i2]], [bgT])
        for sub in range(nsub):
            for q in range(2):
                pt, pb = ps[4 + q]
                for fc in range(NG_):
                    O.mm(pt[:, :], gT[:, fc, sub * 128:(sub + 1) * 128], Wd[:, fc, q * 512:(q + 1) * 512], [bgT, bWd], [pb],
                         start=(fc == 0), stop=(fc == NG_ - 1))
                O.copy("act", og[:, q * 512:(q + 1) * 512], pt[:, :], [pb], [pb, bog])
            rms_rstd(ctx, og, sq, ss[:, 1:2], 128, bog, bsq)
            P.op("dve", lambda e: e.scalar_tensor_tensor(out=og, in0=og, scalar=ss[:, 1:2], in1=gpost, op0=ALU.mult, op1=ALU.mult),
                 reads=[bog, bsq, bG2], writes=[bog])
            O.tt("dve", xt[:, sub, :], xt[:, sub, :], og, ALU.add, [bx, bog], [bx])
            rr = r0 + sub * 128
            dst = dst_s if is_s else dst_p[rr:rr + 128, :]
            P.dma("sp", dst, xt[:, sub, :], reads=[bx])
            if dbgname and dbgname in ctx['dbg']:
                P.dma("sp", ctx['dbg'][dbgname][rr:rr + 128, :], xt[:, sub, :], reads=[bx])
        if (not is_s) and n == len(blocks) - 2:
            for r in range(2):
                P.dma("sp", ctx['o']['cp'][l, r].rearrange("(c p) -> p c", p=128), halo_p[:, :, 0, r], reads=bHp,
                      allow_slow_non_contiguous=True)
    for s_ in range(SPC):
        for r in range(2):
            P.dma("sp", ctx['o']['cs'][l, s_, r].rearrange("(c p) -> p c", p=128), halo_s[:, :, s_, r], reads=bHs,
                  allow_slow_non_contiguous=True)


def phase_kvq(ctx):
    P, O, A, W, CT, ps, psb = (ctx[k] for k in ('P', 'O', 'A', 'W', 'CT', 'ps', 'psb'))
    SEQ = ctx['SEQ']
    idf = A.alloc([128, 128], F32); idb = A.alloc([128, 128], BF16); bI = Buf()
    P.dma("sp", idf, CT['c_ident'], writes=[bI]); O.copy("dve", idb, idf, [bI], [bI])
    Wk, bWk = load_w_bf16(ctx, W['w_k'], 8, D)
    Wv, bWv = load_w_bf16(ctx, W['w_v'], 8, D)
    Wq, bWq = load_w_bf16(ctx, W['w_q'][0], 8, D)
    gkv, bG1 = load_bcast(ctx, W['kv_norm_g'])
    gq, bG2 = load_bcast(ctx, W['b_pre_g'][0])
    xt = A.alloc([128, D], F32); sq = A.alloc([128, D], F32); ss = A.alloc([128, 1], F32)
    xa = A.alloc([128, D], BF16); xb = A.alloc([128, D], BF16)
    aT = A.alloc([128, 8, 128], BF16); bT_ = A.alloc([128, 8, 128], BF16)
    kf = A.alloc([128, D], F32); vf = A.alloc([128, D], F32); vb = A.alloc([128, D], BF16)
    KTt = A.alloc([128, NH, 128], BF16); QTt = A.alloc([128, NH, 128], BF16)
    bx, bsq, bxa, bxb, baT, bbT, bkf, bvf, bvb, bKT, bQT = (Buf() for _ in range(11))
    for (r0, is_s) in token_blocks(ctx):
        P.dma("sp", xt, ctx['xres'][r0:r0 + 128, :], writes=[bx])
        rms_rstd(ctx, xt, sq, ss, 128, bx, bsq)
        P.op("dve", lambda e: e.scalar_tensor_tensor(out=xa, in0=xt, scalar=ss[:, 0:1], in1=gkv, op0=ALU.mult, op1=ALU.mult),
             reads=[bx, bsq, bG1], writes=[bxa])
        P.op("dve", lambda e: e.scalar_tensor_tensor(out=xb, in0=xt, scalar=ss[:, 0:1], in1=gq, op0=ALU.mult, op1=ALU.mult),
             reads=[bx, bsq, bG2], writes=[bxb])
        transpose_block(ctx, xa, idb, aT, [bxa, bI], [baT])
        transpose_block(ctx, xb, idb, bT_, [bxb, bI], [bbT])
        for (Wt, bW, dstf, bdst, q0) in ((Wk, bWk, kf, bkf, 0), (Wv, bWv, vf, bvf, 2)):
            for q in range(2):
                pt, pb = ps[q0 + q]
                for k in range(8):
                    O.mm(pt[:, :], aT[:, k, :], Wt[:, k, q * 512:(q + 1) * 512], [baT, bW], [pb], start=(k == 0), stop=(k == 7))
                O.copy("act", dstf[:, q * 512:(q + 1) * 512], pt[:, :], [pb], [pb, bdst])
        O.copy("dve", vb, vf, [bvf], [bvb])
        P.dma("sp", (ctx['o']['ks'] if is_s else ctx['o']['kp'][r0:r0 + 128, :]), kf, reads=[bkf])
        P.dma("sp", (ctx['o']['vs'] if is_s else ctx['o']['vp'][r0:r0 + 128, :]), vf, reads=[bvf])
        P.dma("sp", ctx['Vbd'][r0:r0 + 128, :], vb, reads=[bvb])
        for h in range(NH):
            pt, pb = ps[4 + h % 2]
            for k in range(8):
                O.mm(pt[:, 0:128], Wk[:, k, h * 128:(h + 1) * 128], aT[:, k, :], [baT, bWk], [pb], start=(k == 0), stop=(k == 7))
            for k in range(8):
                O.mm(pt[:, 128:256], Wq[:, k, h * 128:(h + 1) * 128], bT_[:, k, :], [bbT, bWq], [pb], start=(k == 0), stop=(k == 7))
            O.copy("act", KTt[:, h, :], pt[:, 0:128], [pb], [pb, bKT])
            O.act(QTt[:, h, :], pt[:, 128:256], AF.Copy, [pb], [pb, bQT], scale=HD ** -0.5)
        P.dma("sp", ctx['KTd'][:, :, r0:r0 + 128].rearrange("h p t -> p h t"), KTt, reads=[bKT])
        P.dma("sp", ctx['QTd'][:, :, r0:r0 + 128].rearrange("h p t -> p h t"), QTt, reads=[bQT])


def phase_attn(ctx):
    P, O, A, W, CT, ps, psb = (ctx[k] for k in ('P', 'O', 'A', 'W', 'CT', 'ps', 'psb'))
    SEQ, NBLK, NPG, NKS = ctx['SEQ'], ctx['NBLK'], ctx['NPG'], ctx['NKS']
    lam_init = 0.8 - 0.6 * float(np.exp(-0.3 * 1))
    idf = A.alloc([128, 128], F32); idb = A.alloc([128, 128], BF16); bI = Buf()
    P.dma("sp", idf, CT['c_ident'], writes=[bI]); O.copy("dve", idb, idf, [bI], [bI])
    bK = Buf()
    ones_f = A.alloc([128, 128], F32); ones_b = A.alloc([128, 128], BF16)
    P.op("dve", lambda e: e.memset(ones_f, 1.0), writes=[bK]); O.copy("dve", ones_b, ones_f, [bK], [bK])
    trif = A.alloc([128, 4, 512], F32); tri = A.alloc([128, 4, 512], BF16)
    P.dma("sp", trif, CT['c_tri'], writes=[bK]); O.copy("dve", tri, trif, [bK], [bK])
    ND = max(NBLK, NPG)
    rel = A.alloc([128, ND], F32)
    P.dma("sp", rel, CT['c_rel'][:, 0:ND], writes=[bK])
    lv = [A.alloc([128, HD], F32) for _ in range(4)]
    for t_, nm in zip(lv, ('lam_q1', 'lam_k1', 'lam_q2', 'lam_k2')):
        P.dma("sp", t_, W[nm][0].partition_broadcast(128), writes=[bK])
    l12 = A.alloc([128, 2], F32); nlam = A.alloc([128, 1], F32)
    O.tt("dve", lv[0], lv[0], lv[1], ALU.mult, [bK], [bK]); O.tt("dve", lv[2], lv[2], lv[3], ALU.mult, [bK], [bK])
    P.op("dve", lambda e: e.reduce_sum(out=l12[:, 0:1], in_=lv[0], axis=AX.X), reads=[bK], writes=[bK])
    P.op("dve", lambda e: e.reduce_sum(out=l12[:, 1:2], in_=lv[2], axis=AX.X), reads=[bK], writes=[bK])
    O.act(l12, l12, AF.Exp, [bK], [bK])
    O.tt("dve", nlam, l12[:, 1:2], l12[:, 0:1], ALU.subtract, [bK], [bK])
    O.ts("dve", nlam, nlam, -lam_init, ALU.add, [bK], [bK])
    subg = A.alloc([128, 1], F32)
    P.dma("sp", subg, W['sub_g'].rearrange("o e -> e o"), writes=[bK], allow_slow_non_contiguous=True)
    O.ts("dve", subg, subg, 1.0 - lam_init, ALU.mult, [bK], [bK])
    pT = [A.alloc([128, 512], BF16) for _ in range(3)]; bpT = [Buf() for _ in range(3)]
    rs = A.alloc([128, 512], F32); o1 = A.alloc([128, 512], F32); o2 = A.alloc([128, 512], F32)
    osq = A.alloc([128, 512], F32)
    bfin = Buf()
    state = {'n': 0}

    def attn_core(KTf, Vf, biasf, maskf, chunks, QT, nq, out_ap, rK, wOut):
        for j in range(2):
            for ci, ch in enumerate(chunks):
                nk = ch[1]
                n = state['n']; state['n'] += 1
                sp, sb = ps[n % 2]
                kt = KTf(ch)
                O.mm(sp[0:nk, 0:nq], kt[64 * j:64 * j + 64, 0:nk], QT[64 * j:64 * j + 64, 0:nq], rK, [sb])
                p_, bp = pT[n % 3], bpT[n % 3]
                O.act(p_[0:nk, 0:nq], sp[0:nk, 0:nq], AF.Exp, [sb, bK], [sb, bp], bias=biasf(ch)[0:nk, :])
                m = maskf(ch)
                if m is not None:
                    O.tt("pool", p_[0:nk, 0:nq], p_[0:nk, 0:nq], m[0:nk, 0:nq], ALU.mult, [bp, bK], [bp])
                first, last = ci == 0, ci == len(chunks) - 1
                O.mm(ps[2 + j][0][:, 0:nq], Vf(ch)[0:nk, :], p_[0:nk, 0:nq], rK + [bp], [ps[2 + j][1]], start=first, stop=last)
                O.mm(ps[4 + j][0][:, 0:nq], ones_b[0:nk, :], p_[0:nk, 0:nq], [bp, bK], [ps[4 + j][1]], start=first, stop=last)
        for j, od in enumerate((o1, o2)):
            P.op("dve", lambda e, j=j: e.reciprocal(out=rs[:, 0:nq], in_=ps[4 + j][0][:, 0:nq]), reads=[ps[4 + j][1], bfin], writes=[bfin, ps[4 + j][1]])
            O.tt("dve", od[:, 0:nq], rs[:, 0:nq], ps[2 + j][0][:, 0:nq], ALU.mult, [bfin, ps[2 + j][1]], [bfin, ps[2 + j][1]])
        P.op("dve", lambda e: e.scalar_tensor_tensor(out=o1[:, 0:nq], in0=o2[:, 0:nq], scalar=nlam[:, 0:1], in1=o1[:, 0:nq],
                                                     op0=ALU.mult, op1=ALU.add), reads=[bfin, bK], writes=[bfin])
        O.tt("dve", osq[:, 0:nq], o1[:, 0:nq], o1[:, 0:nq], ALU.mult, [bfin], [bfin])
        n = state['n']; state['n'] += 1
        sp, sb = ps[n % 2]
        O.mm(sp[:, 0:nq], ones_f[:, :], osq[:, 0:nq], [bfin, bK], [sb])
        O.ts("dve", rs[:, 0:nq], sp[:, 0:nq], 1.0 / VD, ALU.mult, [sb, bfin], [bfin, sb], s2=EPS, op1=ALU.add)
        O.act(rs[:, 0:nq], rs[:, 0:nq], AF.Sqrt, [bfin], [bfin])
        P.op("dve", lambda e: e.reciprocal(out=rs[:, 0:nq], in_=rs[:, 0:nq]), reads=[bfin], writes=[bfin])
        P.op("dve", lambda e: e.scalar_tensor_tensor(out=out_ap, in0=o1[:, 0:nq], scalar=subg[:, 0:1], in1=rs[:, 0:nq],
                                                     op0=ALU.mult, op1=ALU.mult), reads=[bfin, bK], writes=wOut + [bfin])

    KT = A.alloc([128, SEQ], BF16); Vh = A.alloc([128, NBLK, 128], BF16); QT = A.alloc([128, SEQ], BF16)
    OT = A.alloc([128, SEQ], BF16); bal = A.alloc([128, ND], F32)
    bKV, bOT, bbal = Buf(), Buf(), Buf()
    for h in range(NH):
        slope = 2.0 ** (-(h + 1))
        P.dma("sp", KT, ctx['KTd'][h, :, 0:SEQ], writes=[bKV])
        P.dma("sp", QT, ctx['QTd'][h, :, 0:SEQ], writes=[bKV])
        P.dma("sp", Vh, ctx['Vbd'][0:SEQ, h * 128:(h + 1) * 128].rearrange("(kc p) e -> p kc e", p=128), writes=[bKV])
        O.ts("dve", bal, rel, slope, ALU.mult, [bK], [bbal])
        w = min(SEQ, {0: 128, 1: 256}.get(h, 512))
        r_ = w // 128
        for QB in range(SEQ // w):
            kmax = r_ * QB + r_ - 1
            chunks = [(kb, 128) for kb in range(kmax + 1)]
            attn_core(lambda ch: KT[:, ch[0] * 128:(ch[0] + 1) * 128], lambda ch: Vh[:, ch[0], :],
                      lambda ch, kmax=kmax: bal[:, kmax - ch[0]:kmax - ch[0] + 1],
                      lambda ch, QB=QB, r_=r_: (tri[:, ch[0] - r_ * QB, :] if ch[0] >= r_ * QB else None),
                      chunks, QT[:, QB * w:(QB + 1) * w], w, OT[:, QB * w:(QB + 1) * w], [bKV, bbal], [bOT])
        P.dma("sp", ctx['OTd'][h, :, 0:SEQ], OT, reads=[bOT])
    P.barrier()
    m0 = A.mark()
    NKC = NPG + 1
    KTs = A.alloc([128, NH, NKS], BF16); Vs = A.alloc([128, NKC, D], BF16)
    qs = A.alloc([128, NH, DEC_S], BF16); ots = A.alloc([128, NH, DEC_S], BF16)
    kp = A.alloc([128, D], F32); vp = A.alloc([128, D], F32); kpb = A.alloc([128, D], BF16)
    bals = A.alloc([128, NH, NKC], F32)
    pti = A.alloc([128, SPC * NPG], I32); ptf = A.alloc([128, SPC * NPG], F32); iot = A.alloc([128, 1], F32)
    bidx, bkp, bvp, bkpb, bKs, bVs, bqs, bots = (Buf() for _ in range(8))
    P.dma("sp", pti, ctx['page_table'].rearrange("s g -> (s g)").partition_broadcast(128), writes=[bidx])
    P.dma("sp", iot, CT['c_iota'], writes=[bidx])
    O.copy("dve", ptf, pti, [bidx], [bidx])
    O.ts("dve", ptf, ptf, float(PAGE), ALU.mult, [bidx], [bidx], s2=iot[:, 0:1], op1=ALU.add)
    O.copy("dve", pti, ptf, [bidx], [bidx])
    for h in range(NH):
        slope = 2.0 ** (-(h + 1))
        for pg in range(NPG):
            O.ts("dve", bals[:, h, pg:pg + 1], rel[:, NPG - 1 - pg:NPG - pg], -8.0, ALU.add, [bK], [bbal], s2=slope, op1=ALU.mult)
        O.ts("dve", bals[:, h, NPG:NPG + 1], rel[:, 0:1], 120.0, ALU.add, [bK], [bbal], s2=slope, op1=ALU.mult)
    for s_ in range(SPC):
        t0 = SEQ + s_ * DEC_S
        for pg in range(NPG):
            col = s_ * NPG + pg
            P.dma_fn("pool", lambda e, col=col: e.indirect_dma_start(
                out=kp, out_offset=None, in_=ctx['cache_k'],
                in_offset=bass.IndirectOffsetOnAxis(ap=pti[:, col:col + 1], axis=0)), reads=[bidx], writes=[bkp])
            P.dma_fn("pool", lambda e, col=col: e.indirect_dma_start(
                out=vp, out_offset=None, in_=ctx['cache_v'],
                in_offset=bass.IndirectOffsetOnAxis(ap=pti[:, col:col + 1], axis=0)), reads=[bidx], writes=[bvp])
            O.copy("dve", kpb, kp, [bkp], [bkpb])
            O.copy("act", Vs[:, pg, :], vp, [bvp], [bVs])
            pt, pb = psb[pg % 2]
            for h in range(NH):
                O.tr(pt[:, h * 128:(h + 1) * 128], kpb[:, h * 128:(h + 1) * 128], idb, [bkpb, bI], [pb])
            O.copy("act", KTs[:, :, pg * 128:(pg + 1) * 128], pt[:, :].rearrange("p (h t) -> p h t", h=NH), [pb], [pb, bKs])
        P.dma("sp", KTs[:, :, NPG * 128:NKS], ctx['KTd'][:, :, t0:t0 + DEC_S].rearrange("h p t -> p h t"), writes=[bKs])
        P.dma("sp", Vs[0:DEC_S, NPG, :], ctx['Vbd'][t0:t0 + DEC_S, :], writes=[bVs])
        P.dma("sp", qs, ctx['QTd'][:, :, t0:t0 + DEC_S].rearrange("h p t -> p h t"), writes=[bqs])
        for h in range(NH):
            chunks = [(pg, 128) for pg in range(NPG)] + [(NPG, DEC_S)]
            attn_core(lambda ch, h=h: KTs[:, h, ch[0] * 128:ch[0] * 128 + ch[1]],
                      lambda ch, h=h: Vs[:, ch[0], h * 128:(h + 1) * 128],
                      lambda ch, h=h: bals[:, h, ch[0]:ch[0] + 1],
                      lambda ch: (tri[:, 0, :] if ch[0] == NPG else None),
                      chunks, qs[:, h, :], DEC_S, ots[:, h, :], [bKs, bVs, bqs, bbal], [bots])
        P.dma("sp", ctx['OTd'][:, :, t0:t0 + DEC_S].rearrange("h p t -> p h t"), ots, reads=[bots])


def phase_oproj(ctx):
    P, O, A, W, CT, ps, psb = (ctx[k] for k in ('P', 'O', 'A', 'W', 'CT', 'ps', 'psb'))
    Wo, bWo = load_w_bf16(ctx, W['w_o'][0], 8, D)
    gp, bG = load_bcast(ctx, W['b_post_g'][0])
    oT = A.alloc([128, NH, 128], BF16); xt = A.alloc([128, D], F32); og = A.alloc([128, D], F32)
    sq = A.alloc([128, D], F32); ss = A.alloc([128, 1], F32)
    boT, bx, bog, bsq = Buf(), Buf(), Buf(), Buf()
    for (r0, is_s) in token_blocks(ctx):
        P.dma("sp", oT, ctx['OTd'][:, :, r0:r0 + 128].rearrange("h p t -> p h t"), writes=[boT])
        P.dma("sp", xt, ctx['xres'][r0:r0 + 128, :], writes=[bx])
        for q in range(2):
            pt, pb = ps[q]
            for h in range(NH):
                O.mm(pt[:, :], oT[:, h, :], Wo[:, h, q * 512:(q + 1) * 512], [boT, bWo], [pb], start=(h == 0), stop=(h == NH - 1))
            O.copy("act", og[:, q * 512:(q + 1) * 512], pt[:, :], [pb], [pb, bog])
        rms_rstd(ctx, og, sq, ss, 128, bog, bsq)
        P.op("dve", lambda e: e.scalar_tensor_tensor(out=og, in0=og, scalar=ss[:, 0:1], in1=gp, op0=ALU.mult, op1=ALU.mult),
             reads=[bog, bsq, bG], writes=[bog])
        O.tt("dve", xt, xt, og, ALU.add, [bx, bog], [bx])
        P.dma("sp", ctx['xres'][r0:r0 + 128, :], xt, reads=[bx])
        if 'x3' in ctx['dbg']:
            P.dma("sp", ctx['dbg']['x3'][r0:r0 + 128, :], xt, reads=[bx])


def kernel(**inputs):
    cfg = Cfg()
    nc = build_program(cfg)
    maps = make_in_maps(cfg, inputs)
    consts = host_consts()
    for m in maps:
        m.update(consts)
    res = run_bass_kernel_spmd(nc, maps, core_ids=list(range(NCORE)))
    return assemble(cfg, res.results)
```

```python
import contextlib
import numpy as np
import concourse.bass as bass
import concourse.mybir as mybir
from concourse.bass_utils import run_bass_kernel_spmd

F32 = mybir.dt.float32
BF16 = mybir.dt.bfloat16
I32 = mybir.dt.int32
U32 = mybir.dt.uint32
AF = mybir.ActivationFunctionType
ALU = mybir.AluOpType
AX = mybir.AxisListType

NDMA_SEM = 16


class Buf:
    __slots__ = ("name", "writer", "readers")

    def __init__(self, name=""):
        self.name = name
        self.writer = None
        self.readers = {}


class Instr:
    __slots__ = ("eng", "fn", "waits", "idx", "is_dma", "dma_slot", "signal", "semval")

    def __init__(self, eng, fn, is_dma):
        self.eng = eng
        self.fn = fn
        self.waits = []
        self.is_dma = is_dma
        self.dma_slot = None
        self.signal = False
        self.semval = None


class Prog:
    COMPUTE = ("pe", "act", "dve", "pool")

    def __init__(self, nc):
        self.nc = nc
        self.streams = {k: [] for k in ("pe", "act", "dve", "pool", "sp")}
        self.dma_count = {"sp": 0, "pool": 0, "act": 0}
        self.all = []

    def _add(self, eng, fn, reads, writes, is_dma=False):
        ins = Instr(eng, fn, is_dma)
        deps = []
        for b in reads:
            if b.writer is not None:
                deps.append(b.writer)
        for b in writes:
            if b.writer is not None:
                deps.append(b.writer)
            for e, ev in b.readers.items():
                deps.append(ev)
        seen = set()
        for d in deps:
            if id(d) in seen or d is ins:
                continue
            seen.add(id(d))
            if (not d.is_dma) and (not is_dma) and d.eng == eng and eng == "pe":
                continue
            ins.waits.append(d)
        if is_dma:
            n = self.dma_count[eng]
            self.dma_count[eng] = n + 1
            ins.dma_slot = n
        ins.idx = len(self.streams[eng])
        self.streams[eng].append(ins)
        for b in reads:
            b.readers[eng if not is_dma else ("dma", id(ins))] = ins
        for b in writes:
            b.writer = ins
            b.readers = {}
        return ins

    def op(self, eng, fn, reads=(), writes=()):
        return self._add(eng, fn, list(reads), list(writes), False)

    def dma(self, eng, out, in_, reads=(), writes=(), **kw):
        return self._add(eng, lambda e: e.dma_start(out=out, in_=in_, **kw), list(reads), list(writes), True)

    def dma_fn(self, eng, fn, reads=(), writes=()):
        return self._add(eng, fn, list(reads), list(writes), True)

    def _tail_events(self):
        evs = []
        for k, s in self.streams.items():
            last = None
            ndma = 0
            for ins in reversed(s):
                if ins.fn is None:
                    continue
                if ins.is_dma:
                    if ndma < NDMA_SEM:
                        evs.append(ins)
                        ndma += 1
                elif last is None:
                    last = ins
                    evs.append(ins)
                if last is not None and ndma >= NDMA_SEM:
                    break
        return evs

    def barrier(self, engs=("pe", "act", "dve", "pool", "sp")):
        evs = self._tail_events()
        for k in engs:
            ins = Instr(k, None, False)
            ins.waits = [d for d in evs if not (d.eng == k and not d.is_dma)]
            ins.idx = len(self.streams[k])
            self.streams[k].append(ins)

    def finish(self):
        self.barrier(engs=("sp",))

    def emit(self):
        nc = self.nc
        for s in self.streams.values():
            for ins in s:
                for d in ins.waits:
                    d.signal = True
        sems = {}
        import contextlib
        stack = contextlib.ExitStack()
        with stack:
            EPOCH = 16000
            nsig = {k: sum(1 for i_ in self.streams[k] if (not i_.is_dma) and i_.signal) for k in self.COMPUTE}
            for k in self.COMPUTE:
                sems[k] = [stack.enter_context(nc.semaphore("s_%s_%d" % (k, j))) for j in range(nsig[k] // EPOCH + 1)]
            dsems = {}
            for q in ("sp", "pool", "act"):
                if self.dma_count[q]:
                    dsems[q] = [stack.enter_context(nc.semaphore("d_%s_%d" % (q, i))) for i in range(NDMA_SEM)]
            for k in self.COMPUTE:
                c = 0
                for ins in self.streams[k]:
                    if ins.is_dma:
                        continue
                    if ins.signal:
                        ins.semval = c
                        c += 1
            block = stack.enter_context(nc.Block())

            def mk(engname):
                stream = self.streams[engname]

                def body(e):
                    waited = {}
                    dma_seen = 0
                    for ins in stream:
                        need = {}
                        for d in ins.waits:
                            if d.is_dma:
                                sem = dsems[d.eng][d.dma_slot % NDMA_SEM]
                                val = 16 * (d.dma_slot // NDMA_SEM + 1)
                            else:
                                sem = sems[d.eng][d.semval // EPOCH]
                                val = d.semval % EPOCH + 1
                            key = id(sem)
                            if need.get(key, (None, 0))[1] < val:
                                need[key] = (sem, val)
                        if ins.is_dma:
                            n = ins.dma_slot
                            if n >= NDMA_SEM:
                                sem = dsems[engname][n % NDMA_SEM]
                                val = 16 * (n // NDMA_SEM)
                                key = id(sem)
                                if need.get(key, (None, 0))[1] < val:
                                    need[key] = (sem, val)
                        for key, (sem, val) in need.items():
                            if waited.get(key, 0) >= val:
                                continue
                            waited[key] = val
                            e.wait_ge(sem, val)
                        if ins.fn is None:
                            continue
                        r = ins.fn(e)
                        if ins.is_dma:
                            r.then_inc(dsems[engname][ins.dma_slot % NDMA_SEM], 16)
                        elif ins.signal:
                            r.then_inc(sems[engname][ins.semval // EPOCH], 1)
                return body

            for engname, deco in (("sp", block.sync), ("pe", block.tensor), ("act", block.scalar),
                                  ("dve", block.vector), ("pool", block.gpsimd)):
                if self.streams[engname]:
                    deco(mk(engname))


D = 1024
NG = 64
GC = 16
NS = 64
T = 8
NH = 8
HD = 64
VD = 128
FF = 2816
F2 = 2 * FF
PAGE = 128
DEC_B = 128
DEC_S = 8
NCORE = 8
SPC = DEC_B // NCORE
PROMPT_CORES = (0, 4)
EPS = 1e-6


class Cfg:
    def __init__(self, seq=8192, past_len=2048, n_pool=2560):
        self.seq = seq
        self.past_len = past_len
        self.n_pages = past_len // PAGE
        self.n_pool = n_pool
        self.nblk = seq // 128


WEIGHT_NAMES = ['a_pre_g', 'a_post_g', 'ssm_lam_re', 'ssm_lam_im', 'ssm_log_dt', 'ssm_b_re', 'ssm_b_im',
                'ssm_c_re', 'ssm_c_im', 'ssm_d', 'glu_w', 'kv_norm_g', 'w_k', 'w_v', 'b_pre_g', 'b_post_g',
                'w_q', 'lam_q1', 'lam_k1', 'lam_q2', 'lam_k2', 'sub_g', 'w_o', 'f_pre_g', 'f_post_g',
                'w_up', 'conv_w', 'conv_b', 'w_down']


def make_in_maps(cfg, inputs):
    f = lambda a: np.ascontiguousarray(np.asarray(a))
    xp = f(inputs['x_prompt'])
    zero_prompt = np.zeros((cfg.seq, D), np.float32)
    weights = {k: f(inputs[k]) for k in WEIGHT_NAMES}
    ck = f(inputs['cache_k']).reshape(cfg.n_pool * PAGE, NH * 2 * HD)
    cv = f(inputs['cache_v']).reshape(cfg.n_pool * PAGE, NH * VD)
    maps = []
    for c in range(NCORE):
        sl = slice(c * SPC, (c + 1) * SPC)
        m = dict(weights)
        m['xp'] = xp[PROMPT_CORES.index(c)] if c in PROMPT_CORES else zero_prompt
        m['xs'] = f(inputs['x_sample'][sl]).reshape(SPC * DEC_S, D)
        m['st_re'] = f(inputs['state_ssm_re'][0, sl])
        m['st_im'] = f(inputs['state_ssm_im'][0, sl])
        m['st_conv'] = f(inputs['state_conv'][:, sl])
        m['cache_k'] = ck
        m['cache_v'] = cv
        m['page_table'] = f(inputs['page_table'][sl])
        maps.append(m)
    return maps


PI = float(np.pi)


def host_consts():
    ident = np.eye(128, dtype=np.float32)
    jj = np.arange(128) // GC
    cmask8 = (jj[None, :] >= jj[:, None]).astype(np.float32)
    pp = np.arange(128)
    qq = np.arange(512)
    tri = np.stack([(d_ * 128 + pp[:, None] <= qq[None, :]) for d_ in range(4)], axis=1).astype(np.float32)
    rel = (pp[:, None] - 127 - 128 * np.arange(64)[None, :]).astype(np.float32)
    iota = pp.astype(np.float32)[:, None]
    return {'c_ident': ident, 'c_cmask8': cmask8, 'c_tri': tri, 'c_rel': rel, 'c_iota': iota}


class Arena:
    def __init__(self, nc, nbytes=206 * 1024):
        self.nc, self.off, self.limit = nc, 0, nbytes
        self.base = nc.alloc_sbuf_tensor("arena", [128, nbytes // 4], F32)
        self.views = {F32: self.base, BF16: self.base.bitcast(BF16), I32: self.base.bitcast(I32)}

    def mark(self):
        return self.off

    def reset(self, m):
        self.off = m

    def alloc(self, shape, dtype, name=None):
        esz = 2 if dtype == BF16 else 4
        n = int(np.prod(shape[1:]))
        off = (self.off + 63) // 64 * 64
        assert off + n * esz <= self.limit, ("SBUF arena overflow", name, off, n * esz)
        self.off = off + n * esz
        ap = self.views[dtype][0:shape[0], off // esz: off // esz + n]
        if len(shape) == 3:
            ap = ap.rearrange("p (a b) -> p a b", a=shape[1])
        elif len(shape) == 4:
            ap = ap.rearrange("p (a b c) -> p a b c", a=shape[1], b=shape[2])
        return ap


class Ops:
    def __init__(self, P):
        self.P = P

    def tt(self, eng, out, a, b, op, r, w):
        return self.P.op(eng, lambda e: e.tensor_tensor(out=out, in0=a, in1=b, op=op), reads=r, writes=w)

    def ts(self, eng, out, a, s1, op0, r, w, s2=None, op1=None):
        if op1 is None:
            return self.P.op(eng, lambda e: e.tensor_scalar(out=out, in0=a, scalar1=s1, scalar2=None, op0=op0), reads=r, writes=w)
        return self.P.op(eng, lambda e: e.tensor_scalar(out=out, in0=a, scalar1=s1, scalar2=s2, op0=op0, op1=op1), reads=r, writes=w)

    def act(self, out, a, func, r, w, scale=1.0, bias=None):
        if bias is None:
            return self.P.op("act", lambda e: e.activation(out=out, in_=a, func=func, scale=scale), reads=r, writes=w)
        return self.P.op("act", lambda e: e.activation(out=out, in_=a, func=func, scale=scale, bias=bias), reads=r, writes=w)

    def copy(self, eng, out, a, r, w):
        if eng == "act":
            return self.P.op(eng, lambda e: e.activation(out=out, in_=a, func=AF.Copy), reads=r, writes=w)
        return self.P.op(eng, lambda e: e.tensor_copy(out=out, in_=a), reads=r, writes=w)

    def mm(self, out, lhsT, rhs, r, w, start=True, stop=True):
        return self.P.op("pe", lambda e: e.matmul(out, lhsT, rhs, start=start, stop=stop), reads=r, writes=w)

    def tr(self, out, in_, ident, r, w):
        return self.P.op("pe", lambda e: e.transpose(out, in_, ident), reads=r, writes=w)


def s5_params(P, O, nc, A, W, CT, ps):
    B = Buf("s5par")
    R, Wr = [B], [B]

    def T_(shape, dt=F32):
        return A.alloc(shape, dt)
    Vr, Vn = T_([64, NG, T * GC], BF16), T_([64, NG, T * GC], BF16)
    Wsr, Wsi = T_([128, NG, NS], BF16), T_([128, NG, NS], BF16)
    Kf = T_([128, NG, T * GC], BF16)
    idt = T_([128, 128])
    pers = {n: (T_([64, NG]), T_([64, NG])) for n in (T, 128)}
    keep = A.mark()
    lre, lim, dtb = T_([64, NG]), T_([64, NG]), T_([64, NG])
    P.dma("sp", lre[:], W['ssm_lam_re'][0].rearrange("g p -> p g"), writes=Wr, allow_slow_non_contiguous=True)
    P.dma("sp", lim[:], W['ssm_lam_im'][0].rearrange("g p -> p g"), writes=Wr, allow_slow_non_contiguous=True)
    P.dma("sp", dtb[:], W['ssm_log_dt'][0, :].partition_broadcast(64), writes=Wr)
    Bre, Bim = T_([64, NG, GC]), T_([64, NG, GC])
    Cre, Cim = T_([64, NG, GC]), T_([64, NG, GC])
    P.dma("sp", Bre[:], W['ssm_b_re'][0].rearrange("g p c -> p g c"), writes=Wr)
    P.dma("sp", Bim[:], W['ssm_b_im'][0].rearrange("g p c -> p g c"), writes=Wr)
    P.dma("sp", Cre[:], W['ssm_c_re'][0].rearrange("g c p -> p g c"), writes=Wr, allow_slow_non_contiguous=True)
    P.dma("sp", Cim[:], W['ssm_c_im'][0].rearrange("g c p -> p g c"), writes=Wr, allow_slow_non_contiguous=True)
    O.act(dtb[:], dtb[:], AF.Exp, R, Wr)
    a_, th = T_([64, NG]), T_([64, NG])
    O.tt("dve", a_[:], lre[:], dtb[:], ALU.mult, R, Wr)
    O.tt("dve", th[:], lim[:], dtb[:], ALU.mult, R, Wr)

    tmp, tmp2, er = T_([64, NG]), T_([64, NG]), T_([64, NG])

    MAGIC = 12582912.0
    C1 = 6.28125
    C2 = 2 * PI - 6.28125
    kq = T_([64, NG])

    def reduce_arg(dst, n, shift):
        O.ts("dve", dst[:], th[:], float(n), ALU.mult, R, Wr, s2=shift, op1=ALU.add)
        O.ts("dve", kq[:], dst[:], 1.0 / (2 * PI), ALU.mult, R, Wr, s2=MAGIC, op1=ALU.add)
        O.ts("dve", kq[:], kq[:], -MAGIC, ALU.add, R, Wr)
        P.op("dve", lambda e: e.scalar_tensor_tensor(out=dst[:], in0=kq[:], scalar=-C1, in1=dst[:],
                                                     op0=ALU.mult, op1=ALU.add), reads=R, writes=Wr)
        P.op("dve", lambda e: e.scalar_tensor_tensor(out=dst[:], in0=kq[:], scalar=-C2, in1=dst[:],
                                                     op0=ALU.mult, op1=ALU.add), reads=R, writes=Wr)

    def power(n):
        pr, pi_ = pers[n] if n in pers else (T_([64, NG]), T_([64, NG]))
        O.act(er[:], a_[:], AF.Exp, R, Wr, scale=float(n))
        reduce_arg(tmp, n, 0.0)
        O.act(pi_[:], tmp[:], AF.Sin, R, Wr)
        reduce_arg(tmp2, n, 0.5 * PI)
        O.act(pr[:], tmp2[:], AF.Sin, R, Wr)
        O.tt("dve", pr[:], pr[:], er[:], ALU.mult, R, Wr)
        O.tt("dve", pi_[:], pi_[:], er[:], ALU.mult, R, Wr)
        return pr, pi_

    pw = {n: power(n) for n in list(range(-T, T + 1)) + [128]}
    l1r, l1i = pw[1]
    nr, den, qr, qi = T_([64, NG]), T_([64, NG]), T_([64, NG]), T_([64, NG])
    O.ts("dve", nr[:], l1r[:], -1.0, ALU.add, R, Wr)
    O.tt("dve", den[:], lre[:], lre[:], ALU.mult, R, Wr)
    O.tt("dve", tmp[:], lim[:], lim[:], ALU.mult, R, Wr)
    O.tt("dve", den[:], den[:], tmp[:], ALU.add, R, Wr)
    P.op("dve", lambda e: e.reciprocal(out=den[:], in_=den[:]), reads=R, writes=Wr)
    O.tt("dve", qr[:], nr[:], lre[:], ALU.mult, R, Wr)
    O.tt("dve", tmp[:], l1i[:], lim[:], ALU.mult, R, Wr)
    O.tt("dve", qr[:], qr[:], tmp[:], ALU.add, R, Wr)
    O.tt("dve", qr[:], qr[:], den[:], ALU.mult, R, Wr)
    O.tt("dve", qi[:], l1i[:], lre[:], ALU.mult, R, Wr)
    O.tt("dve", tmp[:], nr[:], lim[:], ALU.mult, R, Wr)
    O.tt("dve", qi[:], qi[:], tmp[:], ALU.subtract, R, Wr)
    O.tt("dve", qi[:], qi[:], den[:], ALU.mult, R, Wr)

    def bc(t_):
        return t_[:].unsqueeze(2).to_broadcast([64, NG, GC])

    def cmul(or_, oi_, ar, ai, br, bi, t1):
        O.tt("dve", or_, ar, br, ALU.mult, R, Wr)
        O.tt("dve", t1, ai, bi, ALU.mult, R, Wr)
        O.tt("dve", or_, or_, t1, ALU.subtract, R, Wr)
        O.tt("dve", oi_, ar, bi, ALU.mult, R, Wr)
        O.tt("dve", t1, ai, br, ALU.mult, R, Wr)
        O.tt("dve", oi_, oi_, t1, ALU.add, R, Wr)

    t3 = T_([64, NG, GC])
    Bbr, Bbi = T_([64, NG, GC]), T_([64, NG, GC])
    cmul(Bbr[:], Bbi[:], bc(qr), bc(qi), Bre[:], Bim[:], t3[:])
    Sr, Si = T_([64, NG, T, GC]), T_([64, NG, T, GC])
    Xr, Xi = T_([64, NG, T * GC], BF16), T_([64, NG, T * GC], BF16)
    flat = lambda t_: t_[:].rearrange("p g j c -> p g (j c)")
    for j in range(T):
        pr, pi_ = pw[j + 1]
        cmul(Sr[:, :, j, :], Si[:, :, j, :], bc(pr), bc(pi_), Cre[:], Cim[:], t3[:])
    O.copy("dve", Vr[:], flat(Sr), R, Wr)
    O.ts("dve", Vn[:], flat(Si), -1.0, ALU.mult, R, Wr)
    for j in range(T):
        pr, pi_ = pw[-(j + 1)]
        cmul(Sr[:, :, j, :], Si[:, :, j, :], bc(pr), bc(pi_), Bbr[:], Bbi[:], t3[:])
    O.copy("dve", Xr[:], flat(Sr), R, Wr)
    O.copy("dve", Xi[:], flat(Si), R, Wr)
    cm = T_([128, 128])
    P.dma("sp", cm[:], CT['c_cmask8'][:, :], writes=Wr)
    for g in range(NG):
        pt, pb = ps[g % 2]
        O.mm(pt[:, 0:128], Xr[:, g, :], Vr[:, g, :], R, [pb], start=True, stop=False)
        O.mm(pt[:, 0:128], Xi[:, g, :], Vn[:, g, :], R, [pb], start=False, stop=True)
        O.tt("dve", Kf[:, g, :], pt[:, 0:128], cm[:], ALU.mult, R + [pb], Wr + [pb])
    for j in range(T):
        pr, pi_ = pw[T - 1 - j]
        cmul(Sr[:, :, j, :], Si[:, :, j, :], bc(pr), bc(pi_), Bbr[:], Bbi[:], t3[:])
    P.dma("sp", idt[:], CT['c_ident'][:, :], writes=Wr)
    for g in range(NG):
        for k, (src, dst) in enumerate(((Sr, Wsr), (Si, Wsi))):
            pt, pb = ps[(2 * g + k) % 2]
            O.tr(pt[:, 0:64], src[:, g, :, :].rearrange("p j c -> p (j c)"), idt[0:64, 0:64], R, [pb])
            O.copy("act", dst[:, g, :], pt[:, 0:64], R + [pb], Wr + [pb])
    return dict(Vr=Vr, Vn=Vn, Wsr=Wsr, Wsi=Wsi, Kf=Kf, a8=pw[T], a128=pw[128], ident=idt, B=B, keep=keep)


def build_program(cfg, debug=False, stop_after=None):
    nc = bass.Bass("TRN2", target_bir_lowering=False)
    SEQ = cfg.seq
    NBLK = cfg.nblk
    NTOK = SEQ + 128
    NPG = cfg.n_pages
    NKS = NPG * PAGE + DEC_S
    din = lambda n, shp, dt=F32: nc.dram_tensor(n, list(shp), dt, kind="ExternalInput").ap()
    dout = lambda n, shp, dt=F32: nc.dram_tensor(n, list(shp), dt, kind="ExternalOutput").ap()
    dint = lambda n, shp, dt=F32: nc.dram_tensor(n, list(shp), dt, kind="Internal").ap()
    wshape = {'a_pre_g': [1, D], 'a_post_g': [1, D], 'ssm_lam_re': [1, NG, NS], 'ssm_lam_im': [1, NG, NS],
              'ssm_log_dt': [1, NG], 'ssm_b_re': [1, NG, NS, GC], 'ssm_b_im': [1, NG, NS, GC],
              'ssm_c_re': [1, NG, GC, NS], 'ssm_c_im': [1, NG, GC, NS], 'ssm_d': [1, D], 'glu_w': [1, D, 2 * D],
              'kv_norm_g': [D], 'w_k': [D, D], 'w_v': [D, D], 'b_pre_g': [1, D], 'b_post_g': [1, D],
              'w_q': [1, D, D], 'lam_q1': [1, HD], 'lam_k1': [1, HD], 'lam_q2': [1, HD], 'lam_k2': [1, HD],
              'sub_g': [1, VD], 'w_o': [1, D, D], 'f_pre_g': [2, D], 'f_post_g': [2, D], 'w_up': [2, D, F2],
              'conv_w': [2, 3, F2], 'conv_b': [2, F2], 'w_down': [2, FF, D]}
    W = {k: din(k, v) for k, v in wshape.items()}
    CT = {k: din(k, v.shape) for k, v in host_consts().items()}
    xp = din('xp', [SEQ, D]); xs = din('xs', [128, D])
    st_re = din('st_re', [SPC, NG, NS]); st_im = din('st_im', [SPC, NG, NS])
    st_conv = din('st_conv', [2, SPC, 2, F2])
    cache_k = din('cache_k', [cfg.n_pool * PAGE, D]); cache_v = din('cache_v', [cfg.n_pool * PAGE, D])
    page_table = din('page_table', [SPC, NPG], I32)
    o_yp = dout('o_yp', [SEQ, D]); o_ys = dout('o_ys', [128, D])
    o_srp = dout('o_srp', [NG, NS]); o_sip = dout('o_sip', [NG, NS])
    o_cp = dout('o_cp', [2, 2, F2])
    o_kp = dout('o_kp', [SEQ, D]); o_vp = dout('o_vp', [SEQ, D])
    o_srs = dout('o_srs', [SPC, NG, NS]); o_sis = dout('o_sis', [SPC, NG, NS])
    o_cs = dout('o_cs', [2, SPC, 2, F2])
    o_ks = dout('o_ks', [128, D]); o_vs = dout('o_vs', [128, D])
    xres = dint('xres', [NTOK, D])
    zs5 = dint('zs5', [NTOK, D], BF16)
    KTd = dint('KTd', [NH, 128, NTOK], BF16)
    QTd = dint('QTd', [NH, 128, NTOK], BF16)
    OTd = dint('OTd', [NH, 128, NTOK], BF16)
    Vbd = dint('Vbd', [NTOK, D], BF16)
    dbg = {}
    if debug:
        dbg['x1'] = dout('d_x1', [NTOK, D]); dbg['x2'] = dout('d_x2', [NTOK, D]); dbg['x3'] = dout('d_x3', [NTOK, D])

    P = Prog(nc); O = Ops(P); A = Arena(nc)
    st = contextlib.ExitStack()
    with st:
        ps = [(st.enter_context(nc.psum_tensor("ps%d" % i, [128, 512], F32)), Buf()) for i in range(6)]
        psb = [(st.enter_context(nc.psum_tensor("psb%d" % i, [128, 1024], BF16)), Buf()) for i in range(2)]
        ctx = dict(nc=nc, P=P, O=O, A=A, W=W, CT=CT, ps=ps, psb=psb, cfg=cfg, SEQ=SEQ, NBLK=NBLK, NTOK=NTOK,
                   xp=xp, xs=xs, xres=xres, zs5=zs5, KTd=KTd, QTd=QTd, OTd=OTd, Vbd=Vbd, dbg=dbg,
                   st_re=st_re, st_im=st_im, st_conv=st_conv, cache_k=cache_k, cache_v=cache_v,
                   page_table=page_table, NPG=NPG, NKS=NKS,
                   o=dict(yp=o_yp, ys=o_ys, srp=o_srp, sip=o_sip, cp=o_cp, kp=o_kp, vp=o_vp, srs=o_srs,
                          sis=o_sis, cs=o_cs, ks=o_ks, vs=o_vs))
        phases = [("s5", lambda: phase_s5(ctx)), ("glu", lambda: phase_glu(ctx)),
                  ("ffn0", lambda: phase_ffn(ctx, 0, xres[0:SEQ], xres[SEQ:NTOK], 'x2')),
                  ("kvq", lambda: phase_kvq(ctx)), ("attn", lambda: phase_attn(ctx)),
                  ("oproj", lambda: phase_oproj(ctx)),
                  ("ffn1", lambda: phase_ffn(ctx, 1, o_yp, o_ys, None))]
        for name, fn in phases:
            fn()
            P.barrier(); A.reset(0)
            if stop_after == name:
                break
        P.finish()
        P.emit()
    return nc


def rms_rstd(ctx, xt, sq, ss, n, bx, bs):
    P, O = ctx['P'], ctx['O']
    O.tt("dve", sq, xt, xt, ALU.mult, [bx], [bs])
    P.op("dve", lambda e: e.reduce_sum(out=ss, in_=sq, axis=AX.X), reads=[bs], writes=[bs])
    O.ts("dve", ss, ss, 1.0 / xt.shape[-1], ALU.mult, [bs], [bs], s2=EPS, op1=ALU.add)
    O.act(ss, ss, AF.Sqrt, [bs], [bs])
    P.op("dve", lambda e: e.reciprocal(out=ss, in_=ss), reads=[bs], writes=[bs])


def gelu_tanh(ctx, out, x, t1, r, w):
    P, O = ctx['P'], ctx['O']
    O.tt("dve", t1, x, x, ALU.mult, r, w)
    O.ts("dve", t1, t1, 0.044715, ALU.mult, w, w, s2=1.0, op1=ALU.add)
    O.tt("dve", t1, t1, x, ALU.mult, r + w, w)
    O.act(t1, t1, AF.Sigmoid, w, w, scale=1.5957691216057308)
    O.tt("dve", out, t1, x, ALU.mult, r + w, w)


def load_bcast(ctx, vec_ap, n=128, dt=F32):
    t = ctx['A'].alloc([n, vec_ap.shape[-1]], dt)
    b = Buf()
    ctx['P'].dma("sp", t, vec_ap.partition_broadcast(n), writes=[b])
    return t, b


def load_w_bf16(ctx, w_ap, kchunks, ncols):
    t = ctx['A'].alloc([128, kchunks, ncols], BF16)
    b = Buf()
    src = w_ap.rearrange("(k p) n -> p k n", p=128)
    step = max(1, 4096 // ncols)
    for k0 in range(0, kchunks, step):
        k1 = min(kchunks, k0 + step)
        ctx['P'].dma("pool", t[:, k0:k1, :], src[:, k0:k1, :], writes=[b])
    return t, b


def transpose_block(ctx, src_bf, idb, dstT, rb, wb, nk=8, n=128):
    P, O = ctx['P'], ctx['O']
    pt, pb = ctx['psb'][ctx.setdefault('_tb', 0) % 2]
    ctx['_tb'] += 1
    for k in range(nk):
        O.tr(pt[:, k * 128:k * 128 + n], src_bf[0:n, k * 128:(k + 1) * 128], idb[0:n, 0:n], rb, [pb])
    O.copy("act", dstT[:, :, 0:n], pt[:, 0:nk * 128].rearrange("p (k t) -> p k t", k=nk)[:, :, 0:n], [pb], wb + [pb])


def phase_s5(ctx):
    P, O, A, W, CT, ps, psb = (ctx[k] for k in ('P', 'O', 'A', 'W', 'CT', 'ps', 'psb'))
    nc = ctx['nc']
    SEQ, NTOK = ctx['SEQ'], ctx['NTOK']
    tb = s5_params(P, O, nc, A, W, CT, ps[0:2])
    P.barrier(); A.reset(tb['keep'])
    Bt = tb['B']; R = [Bt]
    gb, bG = load_bcast(ctx, W['a_pre_g'][0])
    gd, bD = load_bcast(ctx, W['ssm_d'][0])
    O.tt("dve", gd, gd, gb, ALU.mult, [bG, bD], [bD])
    idb = A.alloc([128, 128], BF16)
    O.copy("dve", idb, tb['ident'], R, R)
    a8r, a8i = tb['a8']
    CS = 64
    UT = A.alloc([128, NG, CS], BF16)
    Er, Ei = A.alloc([64, NG, CS], F32), A.alloc([64, NG, CS], F32)
    sA = [A.alloc([64, NG], F32) for _ in range(4)]
    t1, t2, t3, t4 = (A.alloc([64, NG], F32) for _ in range(4))
    bS = [Buf(), Buf()]
    bT = Buf(); bT2 = Buf()
    hbuf = [(A.alloc([64, CS], BF16), A.alloc([64, CS], BF16), Buf()) for _ in range(2)]
    P.op("dve", lambda e: e.memset(sA[0], 0.0), writes=[bS[0]])
    P.op("dve", lambda e: e.memset(sA[1], 0.0), writes=[bS[0]])
    cur = 0
    xt = A.alloc([128, T, D], F32)
    yt = A.alloc([128, T, D], F32)
    ub = A.alloc([128, NG, T, GC], BF16)
    ss = A.alloc([128, T], F32)
    bx, bs, bu, by, bUT, bE = Buf(), Buf(), Buf(), Buf(), Buf(), Buf()
    segs = [(xv0, min(CS, SEQ // T - xv0), False) for xv0 in range(0, SEQ // T, CS)] + [(0, SPC, True)]
    xpv = ctx['xp'].rearrange("(ch i) d -> ch i d", i=T)
    xsv = ctx['xs'].rearrange("(ch i) d -> ch i d", i=T)
    for (c0, cn, is_s) in segs:
        src = xsv[0:cn] if is_s else xpv[c0:c0 + cn]
        row0 = SEQ if is_s else c0 * T
        P.dma("sp", xt[0:cn], src, writes=[bx])
        P.dma("sp", ctx['xres'][row0:row0 + cn * T, :].rearrange("(ch i) d -> ch i d", i=T), xt[0:cn], reads=[bx])
        xf = xt[0:cn].rearrange("p i d -> p (i d)")
        yf = yt[0:cn].rearrange("p i d -> p (i d)")
        O.tt("dve", yf, xf, xf, ALU.mult, [bx], [by])
        P.op("dve", lambda e, cn=cn: e.reduce_sum(out=ss[0:cn], in_=yt[0:cn], axis=AX.X), reads=[by], writes=[bs])
        O.ts("dve", ss[0:cn], ss[0:cn], 1.0 / D, ALU.mult, [bs], [bs], s2=EPS, op1=ALU.add)
        O.act(ss[0:cn], ss[0:cn], AF.Sqrt, [bs], [bs])
        P.op("dve", lambda e, cn=cn: e.reciprocal(out=ss[0:cn], in_=ss[0:cn]), reads=[bs], writes=[bs])
        for i in range(T):
            P.op("dve", lambda e, i=i, cn=cn: e.scalar_tensor_tensor(
                out=ub[0:cn, :, i, :], in0=xt[0:cn, i, :].rearrange("p (g c) -> p g c", c=GC),
                scalar=ss[0:cn, i:i + 1], in1=gb[0:cn, :].rearrange("p (g c) -> p g c", c=GC),
                op0=ALU.mult, op1=ALU.mult), reads=[bx, bs, bG], writes=[bu])
            P.op("dve", lambda e, i=i, cn=cn: e.scalar_tensor_tensor(
                out=yt[0:cn, i, :], in0=xt[0:cn, i, :], scalar=ss[0:cn, i:i + 1], in1=gd[0:cn, :],
                op0=ALU.mult, op1=ALU.mult), reads=[bx, bs, bD], writes=[by])
        for g in range(NG):
            pt, pb = psb[g % 2]
            O.tr(pt[:, 0:cn], ub[0:cn, g, :, :].rearrange("p j c -> p (j c)"), idb[0:cn, 0:cn], [bu] + R, [pb])
            O.copy("act", UT[:, g, 0:cn], pt[:, 0:cn], [pb], [pb, bUT])
        for g in range(NG):
            for k, (Wt, Et) in enumerate(((tb['Wsr'], Er), (tb['Wsi'], Ei))):
                pt, pb = ps[(2 * g + k) % 2]
                O.mm(pt[0:64, 0:cn], Wt[:, g, :], UT[:, g, 0:cn], [bUT] + R, [pb])
                O.copy("act", Et[:, g, 0:cn], pt[0:64, 0:cn], [pb], [pb, bE])
        if is_s:
            xflat = xt.rearrange("p i d -> p (i d)")
            carve = lambda n_: xflat[0:64, n_ * 1024:(n_ + 1) * 1024].rearrange("p (g s) -> p g s", s=SPC)
            Hr, Hi, Fr, Fi, u1, u2 = (carve(n_) for n_ in range(6))
            bH = bF = bx
            for s_ in range(SPC):
                P.dma("sp", Hr[:, :, s_], ctx['st_re'][s_].rearrange("g p -> p g"), reads=[bu, by], writes=[bH], allow_slow_non_contiguous=True)
                P.dma("sp", Hi[:, :, s_], ctx['st_im'][s_].rearrange("g p -> p g"), reads=[bu, by], writes=[bH], allow_slow_non_contiguous=True)
            ar = a8r.unsqueeze(2).to_broadcast([64, NG, SPC]); ai = a8i.unsqueeze(2).to_broadcast([64, NG, SPC])
            O.tt("dve", u1, Hr, ar, ALU.mult, [bH] + R, [bF]); O.tt("dve", u2, Hi, ai, ALU.mult, [bH] + R, [bF])
            O.tt("dve", u1, u1, u2, ALU.subtract, [bF], [bF]); O.tt("dve", Fr, u1, Er[:, :, 0:cn], ALU.add, [bF, bE], [bF])
            O.tt("dve", u1, Hi, ar, ALU.mult, [bH] + R, [bF]); O.tt("dve", u2, Hr, ai, ALU.mult, [bH] + R, [bF])
            O.tt("dve", u1, u1, u2, ALU.add, [bF], [bF]); O.tt("dve", Fi, u1, Ei[:, :, 0:cn], ALU.add, [bF, bE], [bF])
            for s_ in range(SPC):
                P.dma("sp", ctx['o']['srs'][s_].rearrange("g p -> p g"), Fr[:, :, s_], reads=[bF], allow_slow_non_contiguous=True)
                P.dma("sp", ctx['o']['sis'][s_].rearrange("g p -> p g"), Fi[:, :, s_], reads=[bF], allow_slow_non_contiguous=True)
            Hsr, Hsi, bHH = Hr, Hi, bH
        else:
            for k in range(cn):
                s_r, s_i = sA[2 * cur], sA[2 * cur + 1]
                n_r, n_i = sA[2 * (1 - cur)], sA[2 * (1 - cur) + 1]
                bc_, bn_ = bS[cur], bS[1 - cur]
                O.tt("dve", t1, s_r, a8r, ALU.mult, [bc_] + R, [bT]); O.tt("dve", t2, s_i, a8i, ALU.mult, [bc_] + R, [bT])
                O.tt("dve", t1, t1, t2, ALU.subtract, [bT], [bT]); O.tt("dve", n_r, t1, Er[:, :, k], ALU.add, [bT, bE], [bn_])
                O.tt("pool", t3, s_i, a8r, ALU.mult, [bc_] + R, [bT2]); O.tt("pool", t4, s_r, a8i, ALU.mult, [bc_] + R, [bT2])
                O.tt("pool", t3, t3, t4, ALU.add, [bT2], [bT2]); O.tt("pool", n_i, t3, Ei[:, :, k], ALU.add, [bT2, bE], [bn_])
                O.copy("act", Er[:, :, k], s_r, [bc_, bn_], [bE])
                O.copy("act", Ei[:, :, k], s_i, [bc_, bn_], [bE])
                cur = 1 - cur
            Hsr, Hsi, bHH = Er, Ei, bE
        for g in range(NG):
            pt, pb = ps[2 + g % 4]
            hr_b, hi_b, bhb = hbuf[g % 2]
            O.copy("pool", hr_b[:, 0:cn], Hsr[:, g, 0:cn], [bHH], [bhb])
            O.copy("pool", hi_b[:, 0:cn], Hsi[:, g, 0:cn], [bHH], [bhb])
            O.mm(pt[0:cn, 0:128], UT[:, g, 0:cn], tb['Kf'][:, g, :], [bUT] + R, [pb], start=True, stop=False)
            O.mm(pt[0:cn, 0:128], hr_b[:, 0:cn], tb['Vr'][:, g, :], [bhb] + R, [pb], start=False, stop=False)
            O.mm(pt[0:cn, 0:128], hi_b[:, 0:cn], tb['Vn'][:, g, :], [bhb] + R, [pb], start=False, stop=True)
            O.tt("dve", yt[0:cn, :, g * GC:(g + 1) * GC], yt[0:cn, :, g * GC:(g + 1) * GC],
                 pt[0:cn, 0:128].rearrange("p (i c) -> p i c", c=GC), ALU.add, [by, pb], [by, pb])
        zb = ub[0:cn].rearrange("p g j c -> p (g j c)")
        gelu_tanh(ctx, zb, yf, xf, [by, bUT], [bx, bu])
        P.dma("sp", ctx['zs5'][row0:row0 + cn * T, :].rearrange("(ch i) d -> ch (i d)", i=T), zb, reads=[bu])
    P.dma("sp", ctx['o']['srp'].rearrange("g p -> p g"), sA[2 * cur], reads=[bS[cur]], allow_slow_non_contiguous=True)
    P.dma("sp", ctx['o']['sip'].rearrange("g p -> p g"), sA[2 * cur + 1], reads=[bS[cur]], allow_slow_non_contiguous=True)


def token_blocks(ctx):
    return [(b * 128, False) for b in range(ctx['NBLK'])] + [(ctx['SEQ'], True)]


def phase_glu(ctx):
    P, O, A, W, CT, ps, psb = (ctx[k] for k in ('P', 'O', 'A', 'W', 'CT', 'ps', 'psb'))
    idf = A.alloc([128, 128], F32); idb = A.alloc([128, 128], BF16); bI = Buf()
    P.dma("sp", idf, CT['c_ident'], writes=[bI]); O.copy("dve", idb, idf, [bI], [bI])
    Wg, bW = load_w_bf16(ctx, W['glu_w'][0], 8, 2 * D)
    gp, bG = load_bcast(ctx, W['a_post_g'][0])
    NB = 2
    zb = [A.alloc([128, D], BF16) for _ in range(NB)]; bz = [Buf() for _ in range(NB)]
    zT = [A.alloc([128, 8, 128], BF16) for _ in range(NB)]; bzT = [Buf() for _ in range(NB)]
    xt = [A.alloc([128, D], F32) for _ in range(NB)]; bx = [Buf() for _ in range(NB)]
    sg = A.alloc([128, D], F32); og = A.alloc([128, D], F32); sq = A.alloc([128, D], F32); ss = A.alloc([128, 1], F32)
    bsg, bog, bsq = Buf(), Buf(), Buf()
    for n, (r0, is_s) in enumerate(token_blocks(ctx)):
        i = n % NB
        P.dma("sp", zb[i], ctx['zs5'][r0:r0 + 128, :], writes=[bz[i]])
        P.dma("sp", xt[i], ctx['xres'][r0:r0 + 128, :], writes=[bx[i]])
        transpose_block(ctx, zb[i], idb, zT[i], [bz[i], bI], [bzT[i]])
        for q in range(4):
            pt, pb = ps[q]
            for k in range(8):
                O.mm(pt[:, :], zT[i][:, k, :], Wg[:, k, q * 512:(q + 1) * 512], [bzT[i], bW], [pb], start=(k == 0), stop=(k == 7))
        for q in range(2):
            O.act(sg[:, q * 512:(q + 1) * 512], ps[2 + q][0][:, :], AF.Sigmoid, [ps[2 + q][1]], [bsg, ps[2 + q][1]])
        for q in range(2):
            O.tt("dve", og[:, q * 512:(q + 1) * 512], sg[:, q * 512:(q + 1) * 512], ps[q][0][:, :], ALU.mult,
                 [bsg, ps[q][1]], [bog, ps[q][1]])
        rms_rstd(ctx, og, sq, ss, 128, bog, bsq)
        P.op("dve", lambda e: e.scalar_tensor_tensor(out=og, in0=og, scalar=ss[:, 0:1], in1=gp, op0=ALU.mult, op1=ALU.mult),
             reads=[bog, bsq, bG], writes=[bog])
        O.tt("dve", xt[i], xt[i], og, ALU.add, [bx[i], bog], [bx[i]])
        P.dma("sp", ctx['xres'][r0:r0 + 128, :], xt[i], reads=[bx[i]])
        if 'x1' in ctx['dbg']:
            P.dma("sp", ctx['dbg']['x1'][r0:r0 + 128, :], xt[i], reads=[bx[i]])


def assemble(cfg, R):
    c0, c1 = PROMPT_CORES
    S = cfg.seq
    cat = lambda k: np.concatenate([np.asarray(R[c][k]) for c in range(NCORE)], axis=0)
    y_prompt = np.stack([R[c0]['o_yp'], R[c1]['o_yp']]).astype(np.float32)
    y_sample = cat('o_ys').reshape(DEC_B, DEC_S, D)
    srp = np.stack([R[c0]['o_srp'], R[c1]['o_srp']])[None]
    sip = np.stack([R[c0]['o_sip'], R[c1]['o_sip']])[None]
    conv_p = np.stack([R[c0]['o_cp'], R[c1]['o_cp']], axis=1)
    k_p = np.stack([R[c0]['o_kp'], R[c1]['o_kp']]).reshape(2, S, NH, 2 * HD)
    v_p = np.stack([R[c0]['o_vp'], R[c1]['o_vp']]).reshape(2, S, NH, VD)
    srs = cat('o_srs')[None]; sis = cat('o_sis')[None]
    conv_s = np.concatenate([np.asarray(R[c]['o_cs']) for c in range(NCORE)], axis=1)
    k_s = cat('o_ks').reshape(DEC_B, DEC_S, NH, 2 * HD)
    v_s = cat('o_vs').reshape(DEC_B, DEC_S, NH, VD)
    return tuple(np.ascontiguousarray(a, dtype=np.float32) for a in
                 (y_prompt, y_sample, srp, sip, conv_p, k_p, v_p, srs, sis, conv_s, k_s, v_s))


def phase_ffn(ctx, l, dst_p, dst_s, dbgname):
    P, O, A, W, CT, ps, psb = (ctx[k] for k in ('P', 'O', 'A', 'W', 'CT', 'ps', 'psb'))
    SEQ = ctx['SEQ']
    NC_ = F2 // 128
    NG_ = NC_ // 2
    idf = A.alloc([128, 128], F32); idb = A.alloc([128, 128], BF16); bI = Buf()
    P.dma("sp", idf, CT['c_ident'], writes=[bI]); O.copy("dve", idb, idf, [bI], [bI])
    Wu, bWu = load_w_bf16(ctx, W['w_up'][l], 8, F2)
    Wd, bWd = load_w_bf16(ctx, W['w_down'][l], NG_, D)
    gpre, bG1 = load_bcast(ctx, W['f_pre_g'][l])
    gpost, bG2 = load_bcast(ctx, W['f_post_g'][l])
    cw = A.alloc([128, NC_, 3], F32); cb = A.alloc([128, NC_], F32); bC = Buf()
    for r in range(3):
        P.dma("sp", cw[:, :, r], W['conv_w'][l, r].rearrange("(c p) -> p c", p=128), writes=[bC], allow_slow_non_contiguous=True)
    P.dma("sp", cb, W['conv_b'][l].rearrange("(c p) -> p c", p=128), writes=[bC], allow_slow_non_contiguous=True)
    halo_p = A.alloc([128, NC_, 1, 2], F32); halo_s = A.alloc([128, NC_, SPC, 2], F32)
    bHp = [Buf() for _ in range(NC_)]; bHs = [Buf() for _ in range(NC_)]
    P.op("dve", lambda e: e.memset(halo_p.rearrange("p c s r -> p (c s r)"), 0.0), writes=bHp)
    m_sc = A.mark()
    sc = A.alloc([2 * SPC, F2], F32); bsc = Buf()
    P.dma("sp", sc, ctx['st_conv'][l].rearrange("s r f -> (s r) f"), writes=[bsc])
    for c in range(NC_):
        pt, pb = ps[c % 2]
        O.tr(pt[:, 0:2 * SPC], sc[:, c * 128:(c + 1) * 128], idf[0:2 * SPC, 0:2 * SPC], [bsc, bI], [pb])
        O.copy("act", halo_s[:, c].rearrange("p s r -> p (s r)"), pt[:, 0:2 * SPC], [pb], [pb, bHs[c]])
    P.barrier(); A.reset(m_sc)
    TB = 256
    xt = A.alloc([128, TB // 128, D], F32); sq = A.alloc([128, D], F32); og = A.alloc([128, D], F32)
    xn = A.alloc([128, D], BF16); xT = A.alloc([128, 8, TB], BF16); ss = A.alloc([128, 2], F32)
    bx, bsq, bog, bxn, bxT = Buf(), Buf(), Buf(), Buf(), Buf()
    hb = [A.alloc([128, 2, TB + 2], F32) for _ in range(2)]; bhb = [Buf(), Buf()]
    hc = [A.alloc([128, 2, TB], F32) for _ in range(2)]; bhc = [Buf(), Buf()]
    t1 = A.alloc([128, TB], F32); bt1 = Buf()
    gT = A.alloc([128, NG_, TB], BF16); bgT = Buf()
    blocks = [(r0, False, TB) for r0 in range(0, SEQ, TB)] + [(SEQ, True, 128)]
    for n, (r0, is_s, ntok) in enumerate(blocks):
        nseq, L = (SPC, DEC_S) if is_s else (1, ntok)
        nsub = ntok // 128
        halo, bH = (halo_s, bHs) if is_s else (halo_p, bHp)
        for sub in range(nsub):
            P.dma("sp", xt[:, sub, :], ctx['xres'][r0 + sub * 128:r0 + (sub + 1) * 128, :], writes=[bx])
            rms_rstd(ctx, xt[:, sub, :], sq, ss[:, 0:1], 128, bx, bsq)
            P.op("dve", lambda e, sub=sub: e.scalar_tensor_tensor(out=xn, in0=xt[:, sub, :], scalar=ss[:, 0:1], in1=gpre,
                                                              op0=ALU.mult, op1=ALU.mult), reads=[bx, bsq, bG1], writes=[bxn])
            transpose_block(ctx, xn, idb, xT[:, :, sub * 128:(sub + 1) * 128], [bxn, bI], [bxT])
        for c in range(NG_):
            i2 = c % 2
            hv = hb[i2][:, :, 0:nseq * (L + 2)].rearrange("p j (s l) -> p j s l", l=L + 2)
            hcv = hc[i2][:, :, 0:ntok].rearrange("p j (s l) -> p j s l", l=L)
            for j, cc in enumerate((c, c + NG_)):
                pt, pb = ps[(2 * c + j) % 4]
                for k in range(8):
                    O.mm(pt[:, 0:ntok], Wu[:, k, cc * 128:(cc + 1) * 128], xT[:, k, 0:ntok], [bWu, bxT], [pb], start=(k == 0), stop=(k == 7))
                O.copy("pool", hv[:, j, :, 0:2], halo[:, cc], [bH[cc]], [bhb[i2]])
                O.copy("act", hv[:, j, :, 2:L + 2], pt[:, 0:ntok].rearrange("p (s l) -> p s l", l=L), [pb], [pb, bhb[i2]])
                O.copy("pool", halo[:, cc], hv[:, j, :, L:L + 2], [bhb[i2]], [bH[cc]])
                P.op("dve", lambda e, hv=hv, hcv=hcv, j=j, cc=cc, L=L: e.tensor_scalar(
                    out=hcv[:, j], in0=hv[:, j, :, 2:L + 2], scalar1=cw[:, cc, 2:3], scalar2=cb[:, cc:cc + 1],
                    op0=ALU.mult, op1=ALU.add), reads=[bhb[i2], bC], writes=[bhc[i2]])
                for tap in (1, 0):
                    P.op("dve", lambda e, hv=hv, hcv=hcv, j=j, cc=cc, L=L, tap=tap: e.scalar_tensor_tensor(
                        out=hcv[:, j], in0=hv[:, j, :, tap:tap + L], scalar=cw[:, cc, tap:tap + 1], in1=hcv[:, j],
                        op0=ALU.mult, op1=ALU.add), reads=[bhb[i2], bC, bhc[i2]], writes=[bhc[i2]])
            gelu_tanh(ctx, t1[:, 0:ntok], hc[i2][:, 0, 0:ntok], t1[:, 0:ntok], [bhc[i2]], [bt1])
            O.tt("dve", gT[:, c, 0:ntok], t1[:, 0:ntok], hc[i2][:, 1, 0:ntok], ALU.mult, [bt1, bhc[i2]], [bgT])
        for sub in range(nsub):
            for q in range(2):
                pt, pb = ps[4 + q]
                for fc in range(NG_):
                    O.mm(pt[:, :], gT[:, fc, sub * 128:(sub + 1) * 128], Wd[:, fc, q * 512:(q + 1) * 512], [bgT, bWd], [pb],
                         start=(fc == 0), stop=(fc == NG_ - 1))
                O.copy("act", og[:, q * 512:(q + 1) * 512], pt[:, :], [pb], [pb, bog])
            rms_rstd(ctx, og, sq, ss[:, 1:2], 128, bog, bsq)
            P.op("dve", lambda e: e.scalar_tensor_tensor(out=og, in0=og, scalar=ss[:, 1:2], in1=gpost, op0=ALU.mult, op1=ALU.mult),
                 reads=[bog, bsq, bG2], writes=[bog])
            O.tt("dve", xt[:, sub, :], xt[:, sub, :], og, ALU.add, [bx, bog], [bx])
            rr = r0 + sub * 128
            dst = dst_s if is_s else dst_p[rr:rr + 128, :]
            P.dma("sp", dst, xt[:, sub, :], reads=[bx])
            if dbgname and dbgname in ctx['dbg']:
                P.dma("sp", ctx['dbg'][dbgname][rr:rr + 128, :], xt[:, sub, :], reads=[bx])
        if (not is_s) and n == len(blocks) - 2:
            for r in range(2):
                P.dma("sp", ctx['o']['cp'][l, r].rearrange("(c p) -> p c", p=128), halo_p[:, :, 0, r], reads=bHp,
                      allow_slow_non_contiguous=True)
    for s_ in range(SPC):
        for r in range(2):
            P.dma("sp", ctx['o']['cs'][l, s_, r].rearrange("(c p) -> p c", p=128), halo_s[:, :, s_, r], reads=bHs,
                  allow_slow_non_contiguous=True)


def phase_kvq(ctx):
    P, O, A, W, CT, ps, psb = (ctx[k] for k in ('P', 'O', 'A', 'W', 'CT', 'ps', 'psb'))
    SEQ = ctx['SEQ']
    idf = A.alloc([128, 128], F32); idb = A.alloc([128, 128], BF16); bI = Buf()
    P.dma("sp", idf, CT['c_ident'], writes=[bI]); O.copy("dve", idb, idf, [bI], [bI])
    Wk, bWk = load_w_bf16(ctx, W['w_k'], 8, D)
    Wv, bWv = load_w_bf16(ctx, W['w_v'], 8, D)
    Wq, bWq = load_w_bf16(ctx, W['w_q'][0], 8, D)
    gkv, bG1 = load_bcast(ctx, W['kv_norm_g'])
    gq, bG2 = load_bcast(ctx, W['b_pre_g'][0])
    xt = A.alloc([128, D], F32); sq = A.alloc([128, D], F32); ss = A.alloc([128, 1], F32)
    xa = A.alloc([128, D], BF16); xb = A.alloc([128, D], BF16)
    aT = A.alloc([128, 8, 128], BF16); bT_ = A.alloc([128, 8, 128], BF16)
    kf = A.alloc([128, D], F32); vf = A.alloc([128, D], F32); vb = A.alloc([128, D], BF16)
    KTt = A.alloc([128, NH, 128], BF16); QTt = A.alloc([128, NH, 128], BF16)
    bx, bsq, bxa, bxb, baT, bbT, bkf, bvf, bvb, bKT, bQT = (Buf() for _ in range(11))
    for (r0, is_s) in token_blocks(ctx):
        P.dma("sp", xt, ctx['xres'][r0:r0 + 128, :], writes=[bx])
        rms_rstd(ctx, xt, sq, ss, 128, bx, bsq)
        P.op("dve", lambda e: e.scalar_tensor_tensor(out=xa, in0=xt, scalar=ss[:, 0:1], in1=gkv, op0=ALU.mult, op1=ALU.mult),
             reads=[bx, bsq, bG1], writes=[bxa])
        P.op("dve", lambda e: e.scalar_tensor_tensor(out=xb, in0=xt, scalar=ss[:, 0:1], in1=gq, op0=ALU.mult, op1=ALU.mult),
             reads=[bx, bsq, bG2], writes=[bxb])
        transpose_block(ctx, xa, idb, aT, [bxa, bI], [baT])
        transpose_block(ctx, xb, idb, bT_, [bxb, bI], [bbT])
        for (Wt, bW, dstf, bdst, q0) in ((Wk, bWk, kf, bkf, 0), (Wv, bWv, vf, bvf, 2)):
            for q in range(2):
                pt, pb = ps[q0 + q]
                for k in range(8):
                    O.mm(pt[:, :], aT[:, k, :], Wt[:, k, q * 512:(q + 1) * 512], [baT, bW], [pb], start=(k == 0), stop=(k == 7))
                O.copy("act", dstf[:, q * 512:(q + 1) * 512], pt[:, :], [pb], [pb, bdst])
        O.copy("dve", vb, vf, [bvf], [bvb])
        P.dma("sp", (ctx['o']['ks'] if is_s else ctx['o']['kp'][r0:r0 + 128, :]), kf, reads=[bkf])
        P.dma("sp", (ctx['o']['vs'] if is_s else ctx['o']['vp'][r0:r0 + 128, :]), vf, reads=[bvf])
        P.dma("sp", ctx['Vbd'][r0:r0 + 128, :], vb, reads=[bvb])
        for h in range(NH):
            pt, pb = ps[4 + h % 2]
            for k in range(8):
                O.mm(pt[:, 0:128], Wk[:, k, h * 128:(h + 1) * 128], aT[:, k, :], [baT, bWk], [pb], start=(k == 0), stop=(k == 7))
            for k in range(8):
                O.mm(pt[:, 128:256], Wq[:, k, h * 128:(h + 1) * 128], bT_[:, k, :], [bbT, bWq], [pb], start=(k == 0), stop=(k == 7))
            O.copy("act", KTt[:, h, :], pt[:, 0:128], [pb], [pb, bKT])
            O.act(QTt[:, h, :], pt[:, 128:256], AF.Copy, [pb], [pb, bQT], scale=HD ** -0.5)
        P.dma("sp", ctx['KTd'][:, :, r0:r0 + 128].rearrange("h p t -> p h t"), KTt, reads=[bKT])
        P.dma("sp", ctx['QTd'][:, :, r0:r0 + 128].rearrange("h p t -> p h t"), QTt, reads=[bQT])


def phase_attn(ctx):
    P, O, A, W, CT, ps, psb = (ctx[k] for k in ('P', 'O', 'A', 'W', 'CT', 'ps', 'psb'))
    SEQ, NBLK, NPG, NKS = ctx['SEQ'], ctx['NBLK'], ctx['NPG'], ctx['NKS']
    lam_init = 0.8 - 0.6 * float(np.exp(-0.3 * 1))
    idf = A.alloc([128, 128], F32); idb = A.alloc([128, 128], BF16); bI = Buf()
    P.dma("sp", idf, CT['c_ident'], writes=[bI]); O.copy("dve", idb, idf, [bI], [bI])
    bK = Buf()
    ones_f = A.alloc([128, 128], F32); ones_b = A.alloc([128, 128], BF16)
    P.op("dve", lambda e: e.memset(ones_f, 1.0), writes=[bK]); O.copy("dve", ones_b, ones_f, [bK], [bK])
    trif = A.alloc([128, 4, 512], F32); tri = A.alloc([128, 4, 512], BF16)
    P.dma("sp", trif, CT['c_tri'], writes=[bK]); O.copy("dve", tri, trif, [bK], [bK])
    ND = max(NBLK, NPG)
    rel = A.alloc([128, ND], F32)
    P.dma("sp", rel, CT['c_rel'][:, 0:ND], writes=[bK])
    lv = [A.alloc([128, HD], F32) for _ in range(4)]
    for t_, nm in zip(lv, ('lam_q1', 'lam_k1', 'lam_q2', 'lam_k2')):
        P.dma("sp", t_, W[nm][0].partition_broadcast(128), writes=[bK])
    l12 = A.alloc([128, 2], F32); nlam = A.alloc([128, 1], F32)
    O.tt("dve", lv[0], lv[0], lv[1], ALU.mult, [bK], [bK]); O.tt("dve", lv[2], lv[2], lv[3], ALU.mult, [bK], [bK])
    P.op("dve", lambda e: e.reduce_sum(out=l12[:, 0:1], in_=lv[0], axis=AX.X), reads=[bK], writes=[bK])
    P.op("dve", lambda e: e.reduce_sum(out=l12[:, 1:2], in_=lv[2], axis=AX.X), reads=[bK], writes=[bK])
    O.act(l12, l12, AF.Exp, [bK], [bK])
    O.tt("dve", nlam, l12[:, 1:2], l12[:, 0:1], ALU.subtract, [bK], [bK])
    O.ts("dve", nlam, nlam, -lam_init, ALU.add, [bK], [bK])
    subg = A.alloc([128, 1], F32)
    P.dma("sp", subg, W['sub_g'].rearrange("o e -> e o"), writes=[bK], allow_slow_non_contiguous=True)
    O.ts("dve", subg, subg, 1.0 - lam_init, ALU.mult, [bK], [bK])
    pT = [A.alloc([128, 512], BF16) for _ in range(3)]; bpT = [Buf() for _ in range(3)]
    rs = A.alloc([128, 512], F32); o1 = A.alloc([128, 512], F32); o2 = A.alloc([128, 512], F32)
    osq = A.alloc([128, 512], F32)
    bfin = Buf()
    state = {'n': 0}

    def attn_core(KTf, Vf, biasf, maskf, chunks, QT, nq, out_ap, rK, wOut):
        items = [(j, ci, ch) for j in range(2) for ci, ch in enumerate(chunks)]
        slots = {}

        def emit_S(t):
            j, ci, ch = items[t]
            nk = ch[1]
            n = state['n']; state['n'] += 1
            sp, sb = ps[n % 2]
            kt = KTf(ch)
            O.mm(sp[0:nk, 0:nq], kt[64 * j:64 * j + 64, 0:nk], QT[64 * j:64 * j + 64, 0:nq], rK, [sb])
            slots[t] = (sp, sb, pT[n % 3], bpT[n % 3])

        emit_S(0)
        for t, (j, ci, ch) in enumerate(items):
            if t + 1 < len(items):
                emit_S(t + 1)
            nk = ch[1]
            sp, sb, p_, bp = slots.pop(t)
            O.act(p_[0:nk, 0:nq], sp[0:nk, 0:nq], AF.Exp, [sb, bK], [sb, bp], bias=biasf(ch)[0:nk, :])
            m = maskf(ch)
            if m is not None:
                O.tt("pool", p_[0:nk, 0:nq], p_[0:nk, 0:nq], m[0:nk, 0:nq], ALU.mult, [bp, bK], [bp])
            first, last = ci == 0, ci == len(chunks) - 1
            O.mm(ps[2 + j][0][:, 0:nq], Vf(ch)[0:nk, :], p_[0:nk, 0:nq], rK + [bp], [ps[2 + j][1]], start=first, stop=last)
            O.mm(ps[4 + j][0][:, 0:nq], ones_b[0:nk, :], p_[0:nk, 0:nq], [bp, bK], [ps[4 + j][1]], start=first, stop=last)
        for j, od in enumerate((o1, o2)):
            P.op("dve", lambda e, j=j: e.reciprocal(out=rs[:, 0:nq], in_=ps[4 + j][0][:, 0:nq]), reads=[ps[4 + j][1], bfin], writes=[bfin, ps[4 + j][1]])
            O.tt("dve", od[:, 0:nq], rs[:, 0:nq], ps[2 + j][0][:, 0:nq], ALU.mult, [bfin, ps[2 + j][1]], [bfin, ps[2 + j][1]])
        P.op("dve", lambda e: e.scalar_tensor_tensor(out=o1[:, 0:nq], in0=o2[:, 0:nq], scalar=nlam[:, 0:1], in1=o1[:, 0:nq],
                                                     op0=ALU.mult, op1=ALU.add), reads=[bfin, bK], writes=[bfin])
        O.tt("dve", osq[:, 0:nq], o1[:, 0:nq], o1[:, 0:nq], ALU.mult, [bfin], [bfin])
        n = state['n']; state['n'] += 1
        sp, sb = ps[n % 2]
        O.mm(sp[:, 0:nq], ones_f[:, :], osq[:, 0:nq], [bfin, bK], [sb])
        O.ts("dve", rs[:, 0:nq], sp[:, 0:nq], 1.0 / VD, ALU.mult, [sb, bfin], [bfin, sb], s2=EPS, op1=ALU.add)
        O.act(rs[:, 0:nq], rs[:, 0:nq], AF.Sqrt, [bfin], [bfin])
        P.op("dve", lambda e: e.reciprocal(out=rs[:, 0:nq], in_=rs[:, 0:nq]), reads=[bfin], writes=[bfin])
        P.op("dve", lambda e: e.scalar_tensor_tensor(out=out_ap, in0=o1[:, 0:nq], scalar=subg[:, 0:1], in1=rs[:, 0:nq],
                                                     op0=ALU.mult, op1=ALU.mult), reads=[bfin, bK], writes=wOut + [bfin])

    KT = A.alloc([128, SEQ], BF16); Vh = A.alloc([128, NBLK, 128], BF16); QT = A.alloc([128, SEQ], BF16)
    OT = A.alloc([128, SEQ], BF16); bal = A.alloc([128, ND], F32)
    bKV, bOT, bbal = Buf(), Buf(), Buf()
    for h in range(NH):
        slope = 2.0 ** (-(h + 1))
        P.dma("sp", KT, ctx['KTd'][h, :, 0:SEQ], writes=[bKV])
        P.dma("sp", QT, ctx['QTd'][h, :, 0:SEQ], writes=[bKV])
        P.dma("sp", Vh, ctx['Vbd'][0:SEQ, h * 128:(h + 1) * 128].rearrange("(kc p) e -> p kc e", p=128), writes=[bKV])
        O.ts("dve", bal, rel, slope, ALU.mult, [bK], [bbal])
        w = min(SEQ, {0: 128, 1: 256}.get(h, 512))
        r_ = w // 128
        for QB in range(SEQ // w):
            kmax = r_ * QB + r_ - 1
            chunks = [(kb, 128) for kb in range(kmax + 1)]
            attn_core(lambda ch: KT[:, ch[0] * 128:(ch[0] + 1) * 128], lambda ch: Vh[:, ch[0], :],
                      lambda ch, kmax=kmax: bal[:, kmax - ch[0]:kmax - ch[0] + 1],
                      lambda ch, QB=QB, r_=r_: (tri[:, ch[0] - r_ * QB, :] if ch[0] >= r_ * QB else None),
                      chunks, QT[:, QB * w:(QB + 1) * w], w, OT[:, QB * w:(QB + 1) * w], [bKV, bbal], [bOT])
        P.dma("sp", ctx['OTd'][h, :, 0:SEQ], OT, reads=[bOT])
    P.barrier()
    m0 = A.mark()
    NKC = NPG + 1
    KTs = A.alloc([128, NH, NKS], BF16); Vs = A.alloc([128, NKC, D], BF16)
    qs = A.alloc([128, NH, DEC_S], BF16); ots = A.alloc([128, NH, DEC_S], BF16)
    kp = A.alloc([128, D], F32); vp = A.alloc([128, D], F32); kpb = A.alloc([128, D], BF16)
    bals = A.alloc([128, NH, NKC], F32)
    pti = A.alloc([128, SPC * NPG], I32); ptf = A.alloc([128, SPC * NPG], F32); iot = A.alloc([128, 1], F32)
    bidx, bkp, bvp, bkpb, bKs, bVs, bqs, bots = (Buf() for _ in range(8))
    P.dma("sp", pti, ctx['page_table'].rearrange("s g -> (s g)").partition_broadcast(128), writes=[bidx])
    P.dma("sp", iot, CT['c_iota'], writes=[bidx])
    O.copy("dve", ptf, pti, [bidx], [bidx])
    O.ts("dve", ptf, ptf, float(PAGE), ALU.mult, [bidx], [bidx], s2=iot[:, 0:1], op1=ALU.add)
    O.copy("dve", pti, ptf, [bidx], [bidx])
    for h in range(NH):
        slope = 2.0 ** (-(h + 1))
        for pg in range(NPG):
            O.ts("dve", bals[:, h, pg:pg + 1], rel[:, NPG - 1 - pg:NPG - pg], -8.0, ALU.add, [bK], [bbal], s2=slope, op1=ALU.mult)
        O.ts("dve", bals[:, h, NPG:NPG + 1], rel[:, 0:1], 120.0, ALU.add, [bK], [bbal], s2=slope, op1=ALU.mult)
    for s_ in range(SPC):
        t0 = SEQ + s_ * DEC_S
        for pg in range(NPG):
            col = s_ * NPG + pg
            P.dma_fn("pool", lambda e, col=col: e.indirect_dma_start(
                out=kp, out_offset=None, in_=ctx['cache_k'],
                in_offset=bass.IndirectOffsetOnAxis(ap=pti[:, col:col + 1], axis=0)), reads=[bidx], writes=[bkp])
            P.dma_fn("pool", lambda e, col=col: e.indirect_dma_start(
                out=vp, out_offset=None, in_=ctx['cache_v'],
                in_offset=bass.IndirectOffsetOnAxis(ap=pti[:, col:col + 1], axis=0)), reads=[bidx], writes=[bvp])
            O.copy("dve", kpb, kp, [bkp], [bkpb])
            O.copy("act", Vs[:, pg, :], vp, [bvp], [bVs])
            pt, pb = psb[pg % 2]
            for h in range(NH):
                O.tr(pt[:, h * 128:(h + 1) * 128], kpb[:, h * 128:(h + 1) * 128], idb, [bkpb, bI], [pb])
            O.copy("act", KTs[:, :, pg * 128:(pg + 1) * 128], pt[:, :].rearrange("p (h t) -> p h t", h=NH), [pb], [pb, bKs])
        P.dma("sp", KTs[:, :, NPG * 128:NKS], ctx['KTd'][:, :, t0:t0 + DEC_S].rearrange("h p t -> p h t"), writes=[bKs])
        P.dma("sp", Vs[0:DEC_S, NPG, :], ctx['Vbd'][t0:t0 + DEC_S, :], writes=[bVs])
        P.dma("sp", qs, ctx['QTd'][:, :, t0:t0 + DEC_S].rearrange("h p t -> p h t"), writes=[bqs])
        for h in range(NH):
            chunks = [(pg, 128) for pg in range(NPG)] + [(NPG, DEC_S)]
            attn_core(lambda ch, h=h: KTs[:, h, ch[0] * 128:ch[0] * 128 + ch[1]],
                      lambda ch, h=h: Vs[:, ch[0], h * 128:(h + 1) * 128],
                      lambda ch, h=h: bals[:, h, ch[0]:ch[0] + 1],
                      lambda ch: (tri[:, 0, :] if ch[0] == NPG else None),
                      chunks, qs[:, h, :], DEC_S, ots[:, h, :], [bKs, bVs, bqs, bbal], [bots])
        P.dma("sp", ctx['OTd'][:, :, t0:t0 + DEC_S].rearrange("h p t -> p h t"), ots, reads=[bots])


def phase_oproj(ctx):
    P, O, A, W, CT, ps, psb = (ctx[k] for k in ('P', 'O', 'A', 'W', 'CT', 'ps', 'psb'))
    Wo, bWo = load_w_bf16(ctx, W['w_o'][0], 8, D)
    gp, bG = load_bcast(ctx, W['b_post_g'][0])
    oT = A.alloc([128, NH, 128], BF16); xt = A.alloc([128, D], F32); og = A.alloc([128, D], F32)
    sq = A.alloc([128, D], F32); ss = A.alloc([128, 1], F32)
    boT, bx, bog, bsq = Buf(), Buf(), Buf(), Buf()
    for (r0, is_s) in token_blocks(ctx):
        P.dma("sp", oT, ctx['OTd'][:, :, r0:r0 + 128].rearrange("h p t -> p h t"), writes=[boT])
        P.dma("sp", xt, ctx['xres'][r0:r0 + 128, :], writes=[bx])
        for q in range(2):
            pt, pb = ps[q]
            for h in range(NH):
                O.mm(pt[:, :], oT[:, h, :], Wo[:, h, q * 512:(q + 1) * 512], [boT, bWo], [pb], start=(h == 0), stop=(h == NH - 1))
            O.copy("act", og[:, q * 512:(q + 1) * 512], pt[:, :], [pb], [pb, bog])
        rms_rstd(ctx, og, sq, ss, 128, bog, bsq)
        P.op("dve", lambda e: e.scalar_tensor_tensor(out=og, in0=og, scalar=ss[:, 0:1], in1=gp, op0=ALU.mult, op1=ALU.mult),
             reads=[bog, bsq, bG], writes=[bog])
        O.tt("dve", xt, xt, og, ALU.add, [bx, bog], [bx])
        P.dma("sp", ctx['xres'][r0:r0 + 128, :], xt, reads=[bx])
        if 'x3' in ctx['dbg']:
            P.dma("sp", ctx['dbg']['x3'][r0:r0 + 128, :], xt, reads=[bx])


def kernel(**inputs):
    cfg = Cfg()
    nc = build_program(cfg)
    maps = make_in_maps(cfg, inputs)
    consts = host_consts()
    for m in maps:
        m.update(consts)
    res = run_bass_kernel_spmd(nc, maps, core_ids=list(range(NCORE)))
    return assemble(cfg, res.results)
```
